# Optimizing a Trainium2 kernel written in Bass

```python
import math
import jax, jax.numpy as jnp
from jax import lax
import numpy as np

D_MODEL = 1024
BATCH = 32
SEQ = 256
DEPTH = 2
DEC_BATCH = 4
DEC_SEQ = 4096
PAST_LEN = 512

GRID_W = 64
N_MIXERS = 4
GROUP_WIDTH = D_MODEL // N_MIXERS
HEAD_DIM = 64
N_HEADS = GROUP_WIDTH // HEAD_DIM
WIN_KV_HEADS = 2
WIN_GROUPS = N_HEADS // WIN_KV_HEADS
WINDOW = 128
WIN_BLOCK = 128
Q_BLOCK = 128
DIFF_DIM = HEAD_DIM // 2
REC_CHUNK = 16
FFN_HIDDEN = ((8 * D_MODEL // 3 + 255) // 256) * 256
ROPE_BASE = 10000.0
EPS = 1e-6
MASK_VALUE = -1e30
SPLIT_SIZES = ((GROUP_WIDTH,) * 4
               + (GROUP_WIDTH, WIN_KV_HEADS * HEAD_DIM, WIN_KV_HEADS * HEAD_DIM)
               + (GROUP_WIDTH,) * 3
               + (GROUP_WIDTH,) * 5)
IN_WIDTH = sum(SPLIT_SIZES)
SPLIT_POINTS = tuple(int(s) for s in np.cumsum(SPLIT_SIZES)[:-1])

kernel_name = 'hybrid_prefix_diffusion_step'

F32 = jnp.float32


def rmsnorm(x, g):
    xf = x.astype(F32)
    y = xf * lax.rsqrt(jnp.mean(xf * xf, axis=-1, keepdims=True) + EPS)
    return (y * g.astype(F32)).astype(x.dtype)


def rope_axis(x, pos):
    half = x.shape[-1] // 2
    inv = ROPE_BASE ** (-jnp.arange(half, dtype=F32) / half)
    ang = pos.astype(F32)[:, None] * inv[None, :]
    cos, sin = jnp.cos(ang), jnp.sin(ang)
    xf = x.astype(F32)
    x1, x2 = xf[..., :half], xf[..., half:]
    return jnp.concatenate([x1 * cos - x2 * sin, x2 * cos + x1 * sin], axis=-1).astype(x.dtype)


def rope_2d(x, row, col):
    d = x.shape[-1] // 2
    return jnp.concatenate([rope_axis(x[..., :d], row), rope_axis(x[..., d:], col)], axis=-1)


def grid_positions(n_tokens):
    n_rows = n_tokens // GRID_W
    row = jnp.repeat(jnp.arange(n_rows), GRID_W)
    col = jnp.tile(jnp.arange(GRID_W), n_rows)
    return row, col


def split_heads(x, n):
    b, l, _ = x.shape
    return x.reshape(b, l, n, -1).transpose(0, 2, 1, 3)


def merge_heads(x):
    b, n, l, d = x.shape
    return x.transpose(0, 2, 1, 3).reshape(b, l, n * d)


def chunk_recurrence(q, k, v, log_f, s0):
    b, h, l, dk = q.shape
    dv = v.shape[-1]
    n = l // REC_CHUNK
    shp = lambda a: a.astype(F32).reshape(b, h, n, REC_CHUNK, a.shape[-1])
    qc, kc, vc, lf = shp(q), shp(k), shp(v), shp(log_f)
    cum = jnp.cumsum(lf, axis=3)
    last = cum[:, :, :, -1:, :]
    causal = jnp.tril(jnp.ones((REC_CHUNK, REC_CHUNK), bool))[:, :, None]
    rel = cum[:, :, :, :, None, :] - cum[:, :, :, None, :, :]
    decay = jnp.where(causal, jnp.exp(jnp.where(causal, rel, 0.0)), 0.0)
    attn = jnp.einsum('bhntd,bhnsd,bhntsd->bhnts', qc, kc, decay)
    o_intra = jnp.einsum('bhnts,bhnsv->bhntv', attn, vc)
    u = jnp.einsum('bhnsd,bhnsv->bhndv', kc * jnp.exp(last - cum), vc)
    dec = jnp.exp(last[:, :, :, 0])

    def step(s, inp):
        d_n, u_n = inp
        return d_n[..., None] * s + u_n, s

    s_fin, s_in = lax.scan(step, s0.astype(F32), (jnp.moveaxis(dec, 2, 0), jnp.moveaxis(u, 2, 0)))
    s_in = jnp.moveaxis(s_in, 0, 2)
    o_inter = jnp.einsum('bhntd,bhndv->bhntv', qc * jnp.exp(cum), s_in)
    o = (o_intra + o_inter).reshape(b, h, l, dv).astype(v.dtype)
    return o, s_fin.astype(v.dtype)


def bidir_recurrence(q, k_f, k_b, v, lf_f, lf_b, s0_f, s0_b):
    flip = lambda a: jnp.flip(a, axis=2)
    o_f, s_f = chunk_recurrence(q, k_f, v, lf_f, s0_f)
    o_b, s_b = chunk_recurrence(flip(q), flip(k_b), flip(v), flip(lf_b), s0_b)
    return o_f + flip(o_b), jnp.stack([s_f, s_b], axis=1)


def block_attention(q, k, v, sink=None):
    b, hk, g, lq, d = q.shape
    nb = lq // Q_BLOCK
    qb = jnp.moveaxis(q.reshape(b, hk, g, nb, Q_BLOCK, d), 3, 0)
    kf = k.astype(F32)
    scale = d ** -0.5

    def attend(qi):
        s = jnp.einsum('bhgqd,bhkd->bhgqk', qi.astype(F32), kf) * scale
        if sink is not None:
            s_sink = jnp.broadcast_to(sink.astype(F32)[None, :, :, None, None], s.shape[:-1] + (1,))
            p = jax.nn.softmax(jnp.concatenate([s_sink, s], axis=-1), axis=-1)[..., 1:]
        else:
            p = jax.nn.softmax(s, axis=-1)
        return jnp.einsum('bhgqk,bhkv->bhgqv', p.astype(v.dtype), v)

    o = lax.map(attend, qb)
    return jnp.moveaxis(o, 0, 3).reshape(b, hk, g, lq, v.shape[-1])


def window_attention(q, k, v, k_ctx, v_ctx, sink):
    b, hk, g, l, d = q.shape
    nb = l // WIN_BLOCK
    pad = ((0, 0), (0, 0), (WIN_BLOCK, WIN_BLOCK), (0, 0))

    def bands(a):
        ap = jnp.pad(a, pad).reshape(b, hk, nb + 2, WIN_BLOCK, a.shape[-1])
        return jnp.concatenate([ap[:, :, :-2], ap[:, :, 1:-1], ap[:, :, 2:]], axis=3)

    kb, vb = bands(k), bands(v)
    qb = q.reshape(b, hk, g, nb, WIN_BLOCK, d).astype(F32)
    scale = d ** -0.5
    s_loc = jnp.einsum('bhgnqd,bhnkd->bhgnqk', qb, kb.astype(F32)) * scale
    q_pos = jnp.arange(l).reshape(nb, WIN_BLOCK)
    k_pos = (jnp.arange(nb)[:, None] - 1) * WIN_BLOCK + jnp.arange(3 * WIN_BLOCK)[None, :]
    valid = ((k_pos[:, None, :] >= 0) & (k_pos[:, None, :] < l)
             & (jnp.abs(q_pos[:, :, None] - k_pos[:, None, :]) <= WINDOW))
    s_loc = jnp.where(valid, s_loc, MASK_VALUE)
    s_ctx = jnp.einsum('bhgnqd,bhkd->bhgnqk', qb, k_ctx.astype(F32)) * scale
    s_sink = jnp.broadcast_to(sink.astype(F32)[None, :, :, None, None, None], s_loc.shape[:-1] + (1,))
    probs = jax.nn.softmax(jnp.concatenate([s_sink, s_loc, s_ctx], axis=-1), axis=-1)
    n_loc = 3 * WIN_BLOCK
    p_loc = jnp.where(valid, probs[..., 1:1 + n_loc], 0.0).astype(v.dtype)
    p_ctx = probs[..., 1 + n_loc:].astype(v.dtype)
    o = (jnp.einsum('bhgnqk,bhnkv->bhgnqv', p_loc, vb)
         + jnp.einsum('bhgnqk,bhkv->bhgnqv', p_ctx, v_ctx.astype(v.dtype)))
    return o.reshape(b, hk, g, l, -1)


def hgrn_gate(z, lb):
    lbh = lb.astype(F32).reshape(N_HEADS, 1, HEAD_DIM)
    zf = z.astype(F32)
    f = lbh + (1.0 - lbh) * jax.nn.sigmoid(zf)
    log_f = jnp.log(jnp.maximum(f, 1e-30))
    k = (1.0 - lbh) * jax.nn.sigmoid(-zf)
    return log_f, k


def token_mixing(h, p, lam_init, lb, ctx, pos):
    b, l, _ = h.shape
    latent = ctx is not None
    (rq, rk, rv, rg, wq, wk, wv, dq, dk, dv, hq, hzf, hzb, hi, hg) = jnp.split(
        h @ p['w_in'], SPLIT_POINTS, axis=-1)

    rq = split_heads(rq, N_HEADS)
    rk = split_heads(rk, N_HEADS) * (HEAD_DIM ** -0.5)
    rv = split_heads(rv, N_HEADS)
    if latent:
        rq, rk = rope_2d(rq, *pos), rope_2d(rk, *pos)
    log_gamma = -jnp.exp(p['ret_decay'].astype(F32))
    lg_f = jnp.broadcast_to(log_gamma[0][None, :, None, None], rq.shape)
    lg_b = jnp.broadcast_to(log_gamma[1][None, :, None, None], rq.shape)
    s0 = ctx['ret'] if latent else jnp.zeros((b, 2, N_HEADS, HEAD_DIM, HEAD_DIM), F32)
    o_ret, st_ret = bidir_recurrence(rq, rk, rk, rv, lg_f, lg_b, s0[:, 0], s0[:, 1])
    o_ret = merge_heads(rmsnorm(o_ret, p['ret_norm'].reshape(N_HEADS, 1, HEAD_DIM))) * jax.nn.silu(rg)

    wq = rmsnorm(split_heads(wq, N_HEADS), p['win_qn'])
    wk = rmsnorm(split_heads(wk, WIN_KV_HEADS), p['win_kn'])
    wv = split_heads(wv, WIN_KV_HEADS)
    sink = p['win_sink'].reshape(WIN_KV_HEADS, WIN_GROUPS)
    if latent:
        wq, wk = rope_2d(wq, *pos), rope_2d(wk, *pos)
        o_win = window_attention(wq.reshape(b, WIN_KV_HEADS, WIN_GROUPS, l, HEAD_DIM), wk, wv,
                                 ctx['win_k'], ctx['win_v'], sink)
    else:
        o_win = block_attention(wq.reshape(b, WIN_KV_HEADS, WIN_GROUPS, l, HEAD_DIM), wk, wv, sink)
    o_win = merge_heads(o_win.reshape(b, N_HEADS, l, HEAD_DIM))

    dq = rmsnorm(dq.reshape(b, l, N_HEADS, 2, DIFF_DIM).transpose(0, 2, 3, 1, 4), p['diff_qn'])
    dk = rmsnorm(dk.reshape(b, l, N_HEADS, 2, DIFF_DIM).transpose(0, 2, 3, 1, 4), p['diff_kn'])
    dv = split_heads(dv, N_HEADS)
    if latent:
        dq, dk = rope_2d(dq, *pos), rope_2d(dk, *pos)
        keys = jnp.concatenate([dk, ctx['diff_k'].astype(dk.dtype)], axis=3)
        vals = jnp.concatenate([dv, ctx['diff_v'].astype(dv.dtype)], axis=2)
    else:
        keys, vals = dk, dv
    o1 = block_attention(dq[:, :, 0:1], keys[:, :, 0], vals)
    o2 = block_attention(dq[:, :, 1:2], keys[:, :, 1], vals)
    lq1, lk1, lq2, lk2 = p['diff_lambda'].astype(F32)
    lam = jnp.exp(jnp.sum(lq1 * lk1)) - jnp.exp(jnp.sum(lq2 * lk2)) + lam_init
    o_diff = (o1[:, :, 0].astype(F32) - lam * o2[:, :, 0].astype(F32)).astype(h.dtype)
    o_diff = merge_heads(rmsnorm(o_diff, p['diff_norm'].reshape(N_HEADS, 1, HEAD_DIM))) * (1.0 - lam_init)

    hq = split_heads(hq, N_HEADS)
    hi = split_heads(hi, N_HEADS)
    lf_f, k_f = hgrn_gate(split_heads(hzf, N_HEADS), lb[0])
    lf_b, k_b = hgrn_gate(split_heads(hzb, N_HEADS), lb[1])
    s0 = ctx['hgrn'] if latent else jnp.zeros((b, 2, N_HEADS, HEAD_DIM, HEAD_DIM), F32)
    o_h, st_h = bidir_recurrence(hq, k_f, k_b, hi, lf_f, lf_b, s0[:, 0], s0[:, 1])
    o_h = merge_heads(rmsnorm(o_h, p['hgrn_norm'].reshape(N_HEADS, 1, HEAD_DIM))) * jax.nn.silu(hg)

    out = jnp.concatenate([o_ret, o_win, o_diff, o_h], axis=-1) @ p['w_out']
    new = None if latent else dict(ret=st_ret, win_k=wk, win_v=wv, diff_k=dk, diff_v=dv, hgrn=st_h)
    return out, new


def trunk_layer(x, cond, p, lam_init, lb, ctx, pos):
    mod = (jax.nn.silu(cond) @ p['w_ada'] + p['b_ada']).reshape(cond.shape[0], 6, 1, D_MODEL)
    sh1, sc1, g1, sh2, sc2, g2 = [mod[:, i] for i in range(6)]
    h = rmsnorm(x, p['norm1']) * (1.0 + sc1) + sh1
    mix, new = token_mixing(h, p, lam_init, lb, ctx, pos)
    x = x + g1 * mix
    h = rmsnorm(x, p['norm2']) * (1.0 + sc2) + sh2
    gate, up = jnp.split(h @ p['w_ffn_in'], 2, axis=-1)
    x = x + g2 * ((jax.nn.silu(gate) * up) @ p['w_ffn_out'])
    return x, new


def setup_inputs(seed: int = 0) -> dict:
    key = jax.random.key(seed)
    ks = iter(jax.random.split(key, 40))
    nrm = lambda shape, s: s * jax.random.normal(next(ks), shape, F32)
    gain = lambda shape: 1.0 + nrm(shape, 0.05)
    ret_base = jnp.asarray(np.log(-np.log(1.0 - 2.0 ** (-5.0 - np.arange(N_HEADS)))), F32)
    return {
        'x_prompt': nrm((BATCH, SEQ, D_MODEL), 1.0),
        'x_sample': nrm((DEC_BATCH, DEC_SEQ, D_MODEL), 1.0),
        'state_ret': nrm((DEC_BATCH, DEPTH, 2, N_HEADS, HEAD_DIM, HEAD_DIM), 0.3),
        'cache_win_k': nrm((DEC_BATCH, DEPTH, WIN_KV_HEADS, PAST_LEN, HEAD_DIM), 1.0),
        'cache_win_v': nrm((DEC_BATCH, DEPTH, WIN_KV_HEADS, PAST_LEN, HEAD_DIM), 1.0),
        'cache_diff_k': nrm((DEC_BATCH, DEPTH, N_HEADS, 2, PAST_LEN, DIFF_DIM), 1.0),
        'cache_diff_v': nrm((DEC_BATCH, DEPTH, N_HEADS, PAST_LEN, HEAD_DIM), 1.0),
        'state_hgrn': nrm((DEC_BATCH, DEPTH, 2, N_HEADS, HEAD_DIM, HEAD_DIM), 0.3),
        'c': nrm((DEC_BATCH, D_MODEL), 1.0),
        'c_ctx': nrm((D_MODEL,), 1.0),
        'norm1_g': gain((DEPTH, D_MODEL)),
        'norm2_g': gain((DEPTH, D_MODEL)),
        'w_ada': nrm((DEPTH, D_MODEL, 6 * D_MODEL), 0.5 * D_MODEL ** -0.5),
        'b_ada': nrm((DEPTH, 6 * D_MODEL), 0.01),
        'w_in': nrm((DEPTH, D_MODEL, IN_WIDTH), D_MODEL ** -0.5),
        'ret_decay': ret_base[None, None, :] + nrm((DEPTH, 2, N_HEADS), 0.05),
        'ret_norm_g': gain((DEPTH, GROUP_WIDTH)),
        'win_q_norm': gain((DEPTH, HEAD_DIM)),
        'win_k_norm': gain((DEPTH, HEAD_DIM)),
        'win_sink': nrm((DEPTH, N_HEADS), 0.5),
        'diff_q_norm': gain((DEPTH, DIFF_DIM)),
        'diff_k_norm': gain((DEPTH, DIFF_DIM)),
        'diff_lambda': nrm((DEPTH, 4, DIFF_DIM), 0.1),
        'diff_norm_g': gain((DEPTH, GROUP_WIDTH)),
        'hgrn_lb_logits': nrm((DEPTH, 2, GROUP_WIDTH), 0.5),
        'hgrn_norm_g': gain((DEPTH, GROUP_WIDTH)),
        'w_out': nrm((DEPTH, D_MODEL, D_MODEL), D_MODEL ** -0.5),
        'w_ffn_in': nrm((DEPTH, D_MODEL, 2 * FFN_HIDDEN), D_MODEL ** -0.5),
        'w_ffn_out': nrm((DEPTH, FFN_HIDDEN, D_MODEL), FFN_HIDDEN ** -0.5),
    }


def reference(x_prompt, x_sample, state_ret, cache_win_k, cache_win_v, cache_diff_k, cache_diff_v,
              state_hgrn, c, c_ctx, norm1_g, norm2_g, w_ada, b_ada, w_in, ret_decay, ret_norm_g,
              win_q_norm, win_k_norm, win_sink, diff_q_norm, diff_k_norm, diff_lambda, diff_norm_g,
              hgrn_lb_logits, hgrn_norm_g, w_out, w_ffn_in, w_ffn_out):
    lb_p = jax.nn.softmax(hgrn_lb_logits.astype(F32), axis=0)
    lb_all = jnp.cumsum(lb_p, axis=0) - lb_p
    pos = grid_positions(x_sample.shape[1])
    y_p, y_s = x_prompt, x_sample
    n_ret, n_wk, n_wv, n_dk, n_dv, n_hg = [], [], [], [], [], []
    for l in range(DEPTH):
        p = dict(norm1=norm1_g[l], norm2=norm2_g[l], w_ada=w_ada[l], b_ada=b_ada[l], w_in=w_in[l],
                 ret_decay=ret_decay[l], ret_norm=ret_norm_g[l], win_qn=win_q_norm[l],
                 win_kn=win_k_norm[l], win_sink=win_sink[l], diff_qn=diff_q_norm[l],
                 diff_kn=diff_k_norm[l], diff_lambda=diff_lambda[l], diff_norm=diff_norm_g[l],
                 hgrn_norm=hgrn_norm_g[l], w_out=w_out[l], w_ffn_in=w_ffn_in[l], w_ffn_out=w_ffn_out[l])
        lam_init = 0.8 - 0.6 * math.exp(-0.3 * l)
        y_p, st = trunk_layer(y_p, c_ctx[None, :], p, lam_init, lb_all[l], None, None)
        n_ret.append(st['ret']); n_wk.append(st['win_k']); n_wv.append(st['win_v'])
        n_dk.append(st['diff_k']); n_dv.append(st['diff_v']); n_hg.append(st['hgrn'])
        ctx = dict(ret=state_ret[:, l], win_k=cache_win_k[:, l], win_v=cache_win_v[:, l],
                   diff_k=cache_diff_k[:, l], diff_v=cache_diff_v[:, l], hgrn=state_hgrn[:, l])
        y_s, _ = trunk_layer(y_s, c, p, lam_init, lb_all[l], ctx, pos)
    return (y_p, y_s, jnp.stack(n_ret, axis=1), jnp.stack(n_wk, axis=1), jnp.stack(n_wv, axis=1),
            jnp.stack(n_dk, axis=1), jnp.stack(n_dv, axis=1), jnp.stack(n_hg, axis=1))
```

```python
import numpy as np
import concourse.bass as bass
import concourse.mybir as mybir
from concourse.ap import AP

F32 = mybir.dt.float32
BF16 = mybir.dt.bfloat16
AF = mybir.ActivationFunctionType
ALU = mybir.AluOpType
AX = mybir.AxisListType


class Buf:
    __slots__ = ("name", "writers", "readers")

    def __init__(self, name):
        self.name = name
        self.writers = {}
        self.readers = {}


class V:
    __slots__ = ("buf", "ap")

    def __init__(self, buf, ap):
        self.buf = buf
        self.ap = ap

    def __getitem__(self, key):
        if isinstance(key, tuple) and any(k is None for k in key):
            ap = self.ap[tuple(k for k in key if k is not None)]
            pos = 0
            for k in key:
                if k is None:
                    ap = ap.unsqueeze(pos)
                    pos += 1
                elif isinstance(k, int):
                    pass
                else:
                    pos += 1
            return V(self.buf, ap)
        return V(self.buf, self.ap[key])

    def re(self, _pat, **kw):
        return V(self.buf, self.ap.rearrange(_pat, **kw))

    def bc(self, shape):
        return V(self.buf, self.ap.broadcast_to(shape))

    def raw(self, offset_elems, dims):
        return V(self.buf, AP(self.ap.tensor, self.ap.offset + offset_elems, dims))

    def bitcast(self, dt):
        return V(self.buf, self.ap.bitcast(dt))

    @property
    def shape(self):
        return self.ap.shape


class Comp:
    def __init__(self, name, sem, unit):
        self.name = name
        self.sem = sem
        self.unit = unit
        self.count = 0


class Issuer:
    def __init__(self, name, eng):
        self.name = name
        self.eng = eng
        self.known = {}
        self.prog = []


class Ctx:
    def __init__(self, nc, es, n_dma_slots=8):
        self.nc = nc
        self.es = es
        self.iss = {}
        self.comp = {}
        for nm, eng in (("pe", nc.tensor), ("act", nc.scalar), ("dve", nc.vector),
                        ("pool", nc.gpsimd), ("sp", nc.sync)):
            self.iss[nm] = Issuer(nm, eng)
        for nm in ("pe", "act", "dve", "pool"):
            sem = es.enter_context(nc.semaphore("s_" + nm))
            self.comp[nm] = Comp(nm, sem, 1)
        self.dma_slots = {}
        self.dma_rr = {}
        for q in ("sp", "pool", "act"):
            sl = []
            for k in range(n_dma_slots):
                sem = es.enter_context(nc.semaphore(f"d_{q}{k}"))
                c = Comp(f"d_{q}{k}", sem, 16)
                self.comp[c.name] = c
                sl.append(c)
            self.dma_slots[q] = sl
            self.dma_rr[q] = 0
        self.n_inst = 0
        self.all_bufs = []
        self.immediate = True
        self.uid = 0
        self.rec = None
        self.t_eng = {}
        self.t_buf = {}

    def sb(self, name, shape, dtype=F32, es=None):
        self.uid += 1
        name = f"{name}_{self.uid}"
        t = (es or self.es).enter_context(self.nc.sbuf_tensor(name, list(shape), dtype))
        b = Buf(name)
        return V(b, t.ap())

    def ps(self, name, shape, dtype=F32):
        t = self.es.enter_context(self.nc.psum_tensor(name, list(shape), dtype))
        b = Buf(name)
        return V(b, t.ap())

    def dram(self, name, shape, dtype=F32, kind="Internal"):
        t = self.nc.dram_tensor(name, list(shape), dtype, kind=kind)
        b = Buf(name)
        return V(b, t.ap())

    def _deps(self, reads, writes):
        deps = {}
        for b in reads:
            for k, c in b.writers.items():
                if deps.get(k, 0) < c:
                    deps[k] = c
        for b in writes:
            for k, c in b.writers.items():
                if deps.get(k, 0) < c:
                    deps[k] = c
            for k, c in b.readers.items():
                if deps.get(k, 0) < c:
                    deps[k] = c
        return deps

    def _emit(self, issuer, comp, fn, reads, writes, extra_deps=None, cost=0.3):
        if self.rec is not None:
            self.rec.append(("op", issuer, comp, fn, reads, writes, cost))
            return
        deps = self._deps(reads, writes)
        if extra_deps:
            for k, c in extra_deps.items():
                if deps.get(k, 0) < c:
                    deps[k] = c
        waits = []
        for k, c in deps.items():
            if issuer.name == "pe" and k == "pe":
                continue
            if issuer.known.get(k, 0) < c:
                issuer.known[k] = c
                cp = self.comp[k]
                waits.append((cp.sem, c * cp.unit))
        comp.count += 1
        cnt = comp.count
        sem, unit = comp.sem, comp.unit

        def thunk(eng, waits=waits, fn=fn, sem=sem, unit=unit):
            for s, v in waits:
                eng.wait_ge(s, v)
            fn(eng).then_inc(sem, unit)

        if self.immediate:
            thunk(issuer.eng)
        else:
            issuer.prog.append(thunk)
        for b in writes:
            b.writers = {comp.name: cnt}
            b.readers = {}
        for b in reads:
            if b in writes:
                continue
            if b.readers.get(comp.name, 0) < cnt:
                b.readers[comp.name] = cnt
        self.n_inst += 1

    def op(self, engname, fn, reads, writes):
        rb = [v.buf for v in reads]
        wb = [v.buf for v in writes]
        n = 1
        try:
            for d in writes[0].ap.shape[1:]:
                n *= int(d)
        except Exception:
            n = 256
        if engname == "pe":
            passes = 4 if (reads and reads[0].ap.dtype == F32) else 1
            cost = max(0.06, n * passes / 2400.0) + 0.03
        elif engname == "act":
            cost = 0.2 + n * 0.0008
        elif engname == "dve":
            cost = 0.1 + n * 0.00115
        else:
            cost = 0.1 + n * 0.0023
        self._emit(self.iss[engname], self.comp[engname], fn, rb, wb, cost=cost)

    def dma(self, q, out, in_, **kw):
        if self.rec is not None:
            self.rec.append(("dma", q, out, in_, kw))
            return
        slots = self.dma_slots[q]
        k = self.dma_rr[q]
        self.dma_rr[q] = (k + 1) % len(slots)
        comp = slots[k]
        extra = {comp.name: comp.count} if comp.count > 0 else None
        o, i = out.ap, in_.ap
        self._emit(self.iss[q], comp, lambda e: e.dma_start(out=o, in_=i, **kw),
                   [in_.buf], [out.buf], extra)

    def mm(self, out, lhsT, rhs, start=True, stop=True, extra_reads=(), **kw):
        o, l, r = out.ap, lhsT.ap, rhs.ap
        reads = [lhsT, rhs] + list(extra_reads)
        self.op("pe", lambda e: e.matmul(o, l, r, start=start, stop=stop, **kw), reads, [out])

    def tr(self, out, in_, ident):
        o, i, d = out.ap, in_.ap, ident.ap
        self.op("pe", lambda e: e.transpose(o, i, d), [in_, ident], [out])

    def act(self, out, in_, func, bias=None, scale=None, accum_out=None, eng="act"):
        o, i = out.ap, in_.ap
        reads = [in_]
        writes = [out]
        kw = {}
        if bias is not None:
            if isinstance(bias, V):
                reads.append(bias)
                kw["bias"] = bias.ap
            else:
                kw["bias"] = bias
        if scale is not None:
            if isinstance(scale, V):
                reads.append(scale)
                kw["scale"] = scale.ap
            else:
                kw["scale"] = scale
        if accum_out is not None:
            writes.append(accum_out)
            kw["accum_out"] = accum_out.ap
        self.op(eng, lambda e: e.activation(o, i, func, **kw), reads, writes)

    def tt(self, eng, out, in0, in1, op, extra_reads=()):
        o, a, b = out.ap, in0.ap, in1.ap
        self.op(eng, lambda e: e.tensor_tensor(o, a, b, op), [in0, in1] + list(extra_reads), [out])

    def ts(self, eng, out, in0, s1, op0, s2=None, op1=None, accum_out=None):
        o, a = out.ap, in0.ap
        reads = [in0]
        writes = [out]
        if isinstance(s1, V):
            reads.append(s1)
            s1 = s1.ap
        if isinstance(s2, V):
            reads.append(s2)
            s2 = s2.ap
        kw = {}
        if op1 is not None:
            kw["op1"] = op1
        if accum_out is not None:
            writes.append(accum_out)
            kw["accum_out"] = accum_out.ap
        self.op(eng, lambda e: e.tensor_scalar(o, a, s1, s2, op0, **kw), reads, writes)

    def stt(self, out, in0, scalar, in1, op0, op1):
        o, a, b = out.ap, in0.ap, in1.ap
        reads = [in0, in1]
        if isinstance(scalar, V):
            reads.append(scalar)
            scalar = scalar.ap
        self.op("dve", lambda e: e.scalar_tensor_tensor(o, a, scalar, b, op0, op1), reads, [out])

    def copy(self, eng, out, in_):
        o, i = out.ap, in_.ap
        if eng == "act":
            self.op(eng, lambda e: e.copy(o, i), [in_], [out])
        else:
            self.op(eng, lambda e: e.tensor_copy(o, i), [in_], [out])

    def memset(self, eng, out, val):
        o = out.ap
        self.op(eng, lambda e: e.memset(o, val), [], [out])

    def reduce(self, eng, out, in_, op, axis=AX.X):
        o, i = out.ap, in_.ap
        self.op(eng, lambda e: e.tensor_reduce(o, i, axis, op), [in_], [out])

    def recip(self, out, in_):
        o, i = out.ap, in_.ap
        self.op("dve", lambda e: e.reciprocal(o, i), [in_], [out])

    def record(self, gen):
        groups = []
        done = False
        while not done:
            self.rec = []
            try:
                next(gen)
            except StopIteration:
                done = True
            if self.rec:
                groups.append(self.rec)
        self.rec = None
        return groups

    def _sim(self, op, commit):
        if op[0] == "dma":
            _, q, out, in_, kw = op
            eng, rb, wb, cost, lat = q, [in_.buf], [out.buf], 0.1, 2.2
        else:
            _, issuer, comp, fn, rb, wb, cost = op
            eng, lat = issuer.name, 0.15
        st = self.t_eng.get(eng, 0.0)
        for b in rb:
            st = max(st, self.t_buf.get(b, 0.0))
        for b in wb:
            st = max(st, self.t_buf.get(b, 0.0))
        if commit:
            self.t_eng[eng] = st + cost
            for b in wb:
                self.t_buf[b] = st + max(cost, 0.0) + lat
        return st

    def schedule(self, chains):
        idx = [0] * len(chains)
        while True:
            best = None
            for i, ch in enumerate(chains):
                if idx[i] >= len(ch):
                    continue
                st = self._sim(ch[idx[i]][0], False)
                if best is None or st < best[0]:
                    best = (st, i)
            if best is None:
                break
            i = best[1]
            grp = chains[i][idx[i]]
            idx[i] += 1
            for op in grp:
                self._sim(op, True)
                if op[0] == "dma":
                    self.dma(op[1], op[2], op[3], **op[4])
                else:
                    self._emit(op[1], op[2], op[3], op[4], op[5], cost=op[6])

    def barrier(self):
        for nm, issuer in self.iss.items():
            waits = []
            for k, cp in self.comp.items():
                if cp.count > 0 and issuer.known.get(k, 0) < cp.count:
                    if nm == "pe" and k == "pe":
                        continue
                    issuer.known[k] = cp.count
                    waits.append((cp.sem, cp.count * cp.unit))

            def thunk(eng, waits=waits):
                for s, v in waits:
                    eng.wait_ge(s, v)
            if self.immediate:
                thunk(issuer.eng)
            else:
                issuer.prog.append(thunk)

    def finish(self):
        sp = self.iss["sp"]
        waits = []
        for k, cp in self.comp.items():
            if cp.count > 0 and sp.known.get(k, 0) < cp.count:
                waits.append((cp.sem, cp.count * cp.unit))

        def fin(eng, waits=waits):
            for s, v in waits:
                eng.wait_ge(s, v)

        if self.immediate:
            fin(sp.eng)
            return
        sp.prog.append(fin)
        nc = self.nc
        with nc.Block() as block:
            @block.sync
            def _(e):
                for t in self.iss["sp"].prog:
                    t(e)

            @block.tensor
            def _(e):
                for t in self.iss["pe"].prog:
                    t(e)

            @block.scalar
            def _(e):
                for t in self.iss["act"].prog:
                    t(e)

            @block.vector
            def _(e):
                for t in self.iss["dve"].prog:
                    t(e)

            @block.gpsimd
            def _(e):
                for t in self.iss["pool"].prog:
                    t(e)
import math
from contextlib import ExitStack
from concourse.bass_utils import run_bass_kernel_spmd

D = 1024; LS = 4096; LP = 256; NPS = 4; NT = 40; NTOK = 5120; DEPTH = 2
INW = 3584; FH = 2816; PAST = 512; EPS = 1e-6
SEQS = [(0, 32, 0, True)] + [(32 + 2 * s, 2, 1, False) for s in range(NPS)]
C_RQ, C_RK, C_RV, C_RG, C_WQ, C_WK, C_WV, C_DQ, C_DK, C_DV, C_HQ, C_HZ, C_HI, C_HG = (
    0, 256, 512, 768, 1024, 1280, 1408, 1536, 1792, 2048, 2304, 2560, 3072, 3328)

IN_SPECS = [
    ("xs", [LS, D]), ("xp", [NPS * LP, D]), ("sret", [2, 2, 4, 64, 64]), ("shg", [2, 2, 4, 64, 64]),
    ("cwk", [2, 2, 512, 64]), ("cwv", [2, 2, 512, 64]), ("cdk", [2, 4, 2, 512, 32]), ("cdv", [2, 4, 512, 64]),
    ("cs", [D]), ("cctx", [D]), ("norm1_g", [2, D]), ("norm2_g", [2, D]), ("w_ada", [2, D, 6 * D]),
    ("b_ada", [2, 6 * D]), ("w_in", [2, D, INW]), ("ret_decay", [2, 8]), ("ret_norm_g", [2, 256]),
    ("win_q_norm", [2, 64]), ("win_k_norm", [2, 64]), ("win_sink", [2, 4]), ("diff_q_norm", [2, 32]),
    ("diff_k_norm", [2, 32]), ("diff_lambda", [2, 128]), ("diff_norm_g", [2, 256]),
    ("hgrn_lb_logits", [2, 512]), ("hgrn_norm_g", [2, 256]), ("w_out", [2, D, D]),
    ("w_ffn_in", [2, D, 2 * FH]), ("w_ffn_out", [2, FH, D]),
    ("ident", [128, 128]), ("rope64", [LS, 128]), ("rope32", [LS, 64]), ("tmask", [128, 3, 128]),
    ("chunkind", [128, 4]), ("wmask", [128, 4, 512]),
]
OUT_SPECS = [
    ("yp", [NPS * LP, D]), ("ys", [LS, D]), ("o_sret", [NPS, 2, 2, 4, 64, 64]), ("o_wk", [NPS, 2, 2, LP, 64]),
    ("o_wv", [NPS, 2, 2, LP, 64]), ("o_dk", [NPS, 2, 4, 2, LP, 32]), ("o_dv", [NPS, 2, 4, LP, 64]),
    ("o_shg", [NPS, 2, 2, 4, 64, 64]),
]


class _Stop(Exception):
    pass


def build(debug=False, nlayers=DEPTH, stage=99):
    nc = bass.Bass("TRN2", target_bir_lowering=False)
    I = {}
    for nm, shp in IN_SPECS:
        I[nm] = V(Buf(nm), nc.dram_tensor(nm, shp, F32, kind="ExternalInput").ap())
    O = {}
    for nm, shp in OUT_SPECS:
        O[nm] = V(Buf(nm), nc.dram_tensor(nm, shp, F32, kind="ExternalOutput").ap())
    es = ExitStack()
    if True:
        cx = Ctx(nc, es, n_dma_slots=16)
        dk = "ExternalOutput" if debug else "Internal"
        X1 = cx.dram("X1", [NTOK, D], F32, kind=dk)
        X2 = cx.dram("X2", [NTOK, D], F32, kind=dk)
        OCAT = cx.dram("OCAT", [NTOK, D], BF16, kind=dk)
        QKT = {m: cx.dram(f"QKT{m}", [NT, 128, 8, 128], BF16) for m in (0, 1)}
        UU = {m: cx.dram(f"UU{m}", [NT, 128, 1024], BF16) for m in (0, 1)}
        VV = {m: cx.dram(f"VV{m}", [NT, 128, 256], BF16) for m in (0, 1)}
        GG = {m: cx.dram(f"GG{m}", [NT, 128, 256], BF16) for m in (0, 1)}
        DECD = cx.dram("DECD", [NT, 128, 16], F32)
        QTB = cx.dram("QTB", [128, NTOK // 256, 2, 256], BF16)
        KTB = cx.dram("KTB", [128, NTOK], BF16)
        VB = cx.dram("VB", [NTOK, 128], BF16)
        QTC = cx.dram("QTC", [128, 2, NTOK], BF16)
        KTC = cx.dram("KTC", [128, 2, NTOK], BF16)
        VC = cx.dram("VC", [NTOK, 256], BF16)

        ps = [cx.ps(f"ps{i}", [128, 512]) for i in range(8)]

        def psb(i):
            return V(ps[i].buf, ps[i].ap.tensor.bitcast(BF16).ap())

        ident = cx.sb("ident", [128, 128]); identb = cx.sb("identb", [128, 128], BF16)
        tmask = cx.sb("tmask", [128, 3, 128]); tmaskb = cx.sb("tmaskb", [128, 2, 128], BF16)
        chunkind = cx.sb("chunkind", [128, 4]); wmaskb = cx.sb("wmaskb", [128, 4, 512], BF16)
        cx.dma("sp", ident, I["ident"]); cx.dma("sp", tmask, I["tmask"]); cx.dma("sp", chunkind, I["chunkind"])
        cx.dma("pool", wmaskb, I["wmask"])
        cx.copy("dve", identb, ident)
        cx.copy("dve", tmaskb, tmask[:, 0:2, :])
        cT = cx.sb("cT", [128, 8, 2]); sil = cx.sb("sil", [128, 8, 2], BF16)
        cx.dma("sp", cT[:, :, 0], I["cs"].re("(kc p) -> p kc", p=128), allow_slow_non_contiguous=True)
        cx.dma("sp", cT[:, :, 1], I["cctx"].re("(kc p) -> p kc", p=128), allow_slow_non_contiguous=True)
        cx.act(sil, cT, AF.Silu)

        def bcast_load(dst, src_ap_v):
            cx.dma("sp", dst, V(src_ap_v.buf, src_ap_v.ap.partition_broadcast(128)))

        def rstd_from_ss(dst, ss, n, scope):
            cx.act(dst, ss, AF.Ln, scale=1.0 / n, bias=EPS)
            cx.act(dst, dst, AF.Exp, scale=-0.5)

        def chk(sidx):
            if stage == sidx:
                raise _Stop()

        for l in range(nlayers):
          try:
            lam_init = 0.8 - 0.6 * math.exp(-0.3 * l)
            last = (l == DEPTH - 1)
            with ExitStack() as LSC:
                A1 = cx.sb("A1", [128, 8, 2], es=LSC); SH1 = cx.sb("SH1", [128, 8, 2], es=LSC)
                A2 = cx.sb("A2", [128, 8, 2], es=LSC); SH2 = cx.sb("SH2", [128, 8, 2], es=LSC)
                G1bc = cx.sb("G1bc", [128, 2, D], es=LSC); G2bc = cx.sb("G2bc", [128, 2, D], es=LSC)
                with ExitStack() as S:
                    wbuf = [cx.sb("wada", [128, 8, 512], BF16, es=S) for _ in range(2)]
                    badaT = cx.sb("badaT", [128, 48], es=S)
                    bbc = cx.sb("bbc", [128, 2, D], es=S)
                    n1T = cx.sb("n1T", [128, 8], es=S); n2T = cx.sb("n2T", [128, 8], es=S)
                    modT = cx.sb("modT", [128, 4, 8, 2], es=S)
                    silrep = cx.sb("silrep", [128, 8, 2, 128], BF16, es=S)
                    cx.copy("dve", silrep, sil[:, :, :, None].bc([128, 8, 2, 128]))
                    cx.dma("sp", badaT, I["b_ada"][l].re("(c p) -> p c", p=128), allow_slow_non_contiguous=True)
                    cx.dma("sp", n1T, I["norm1_g"][l].re("(c p) -> p c", p=128), allow_slow_non_contiguous=True)
                    cx.dma("sp", n2T, I["norm2_g"][l].re("(c p) -> p c", p=128), allow_slow_non_contiguous=True)
                    bcast_load(bbc[:, 0, :], I["b_ada"][l, 2 * D:3 * D])
                    bcast_load(bbc[:, 1, :], I["b_ada"][l, 5 * D:6 * D])
                    for j in range(12):
                        wb = wbuf[j % 2]
                        cx.dma("pool", wb, I["w_ada"][l, :, j * 512:(j + 1) * 512].re("(kc p) n -> p kc n", p=128))
                        v, half = j // 2, j % 2
                        if v in (2, 5):
                            gi = 0 if v == 2 else 1
                            for g in range(2):
                                pt = ps[(j + g) % 2]
                                for kc in range(8):
                                    cx.mm(pt, silrep[:, kc, g, :], wb[:, kc, :], start=(kc == 0), stop=(kc == 7))
                                dst = (G1bc if gi == 0 else G2bc)[:, g, half * 512:(half + 1) * 512]
                                cx.tt("dve", dst, pt, bbc[:, gi, half * 512:(half + 1) * 512], ALU.add)
                        else:
                            vi = {0: 0, 1: 1, 3: 2, 4: 3}[v]
                            pt = ps[2 + (j % 2)]
                            for oc in range(4):
                                for kc in range(8):
                                    cx.mm(pt[:, oc * 2:(oc + 1) * 2], wb[:, kc, oc * 128:(oc + 1) * 128], sil[:, kc, :],
                                          start=(kc == 0), stop=(kc == 7))
                            c0 = v * 8 + half * 4
                            cx.tt("dve", modT[:, vi, half * 4:(half + 1) * 4, :], pt[:, 0:8].re("p (c g) -> p c g", g=2),
                                  badaT[:, c0:c0 + 4, None].bc([128, 4, 2]), ALU.add)
                    for (Adst, SHdst, nT, ish, isc) in ((A1, SH1, n1T, 0, 1), (A2, SH2, n2T, 2, 3)):
                        cx.ts("dve", Adst, modT[:, isc], 1.0, ALU.add)
                        cx.tt("dve", Adst, Adst, nT[:, :, None].bc([128, 8, 2]), ALU.mult)
                        cx.copy("dve", SHdst, modT[:, ish])
                cx.barrier()
                chk(0)

                with ExitStack() as S:
                    win = cx.sb("win", [128, 8, INW], BF16, es=S)
                    for blk in range(7):
                        cx.dma("pool", win[:, :, blk * 512:(blk + 1) * 512],
                               I["w_in"][l, :, blk * 512:(blk + 1) * 512].re("(kc p) n -> p kc n", p=128))
                    gw = cx.sb("gw", [128, 6, 64], es=S); gd = cx.sb("gd", [128, 16, 32], es=S)
                    t64 = cx.sb("t64", [128, 2, 64], es=S); t32 = cx.sb("t32", [128, 2, 32], es=S)
                    bcast_load(t64[:, 0, :], I["win_q_norm"][l]); bcast_load(t64[:, 1, :], I["win_k_norm"][l])
                    bcast_load(t32[:, 0, :], I["diff_q_norm"][l]); bcast_load(t32[:, 1, :], I["diff_k_norm"][l])
                    cx.copy("dve", gw[:, 0:4, :], t64[:, 0:1, :].bc([128, 4, 64]))
                    cx.copy("dve", gw[:, 4:6, :], t64[:, 1:2, :].bc([128, 2, 64]))
                    cx.copy("dve", gd[:, 0:8, :], t32[:, 0:1, :].bc([128, 8, 32]))
                    cx.copy("dve", gd[:, 8:16, :], t32[:, 1:2, :].bc([128, 8, 32]))
                    gret = cx.sb("gret", [128, 256], es=S); ghg = cx.sb("ghg", [128, 256], es=S)
                    bcast_load(gret, I["ret_norm_g"][l]); bcast_load(ghg, I["hgrn_norm_g"][l])
                    LB = cx.sb("LB", [128, 512], es=S); OMLB = cx.sb("OMLB", [128, 512], es=S)
                    if l == 0:
                        cx.memset("pool", LB, 0.0); cx.memset("pool", OMLB, 1.0)
                    else:
                      with ExitStack() as S3:
                        lg = cx.sb("lblog", [128, 2, 512], es=S3); mx = cx.sb("lbmx", [128, 512], es=S3)
                        for ll in range(2):
                            bcast_load(lg[:, ll, :], I["hgrn_lb_logits"][ll])
                        cx.tt("dve", mx, lg[:, 0, :], lg[:, 1, :], ALU.max)
                        cx.tt("dve", lg, lg, mx[:, None, :].bc([128, 2, 512]), ALU.subtract)
                        cx.act(lg, lg, AF.Exp)
                        cx.tt("dve", mx, lg[:, 0, :], lg[:, 1, :], ALU.add)
                        cx.recip(mx, mx)
                        cx.tt("dve", LB.re("p (h r d) -> p r h d", h=4, r=2), lg[:, 0, :].re("p (r h d) -> p r h d", r=2, h=4),
                              mx.re("p (r h d) -> p r h d", r=2, h=4), ALU.mult)
                        cx.ts("dve", OMLB, LB, -1.0, ALU.mult, 1.0, ALU.add)
                        cx.barrier()

                    cumT = cx.sb("cumT", [128, 512], es=S)

                    def tables(lf, E, Ei, W, DEC, scope):
                        cum = cumT
                        cx.mm(ps[4], tmask[:, 0, :], lf)
                        cx.mm(ps[5], tmask[:, 1, :], lf)
                        cx.mm(ps[6], tmask[:, 2, :], lf)
                        c4 = cum.re("p (h r d) -> p h r d", h=4, r=2)
                        cx.copy("act", c4[:, :, 0, :], ps[4].re("p (h r d) -> p h r d", h=4, r=2)[:, :, 0, :])
                        cx.copy("act", c4[:, :, 1, :], ps[5].re("p (h r d) -> p h r d", h=4, r=2)[:, :, 1, :])
                        cx.tt("dve", W, ps[6], cum, ALU.subtract)
                        if DEC is not None:
                            for h in range(4):
                                cx.mm(ps[7][:, h * 4:(h + 1) * 4], lf[:, h * 128:(h + 1) * 128], chunkind)
                            cx.act(DEC, ps[7][:, 0:16], AF.Exp)
                        yield
                        cx.act(E, cum, AF.Exp)
                        yield
                        cx.act(Ei, cum, AF.Exp, scale=-1.0)
                        yield
                        cx.act(W, W, AF.Exp)
                        yield

                    Er = cx.sb("Er", [128, 512], es=S); Eir = cx.sb("Eir", [128, 512], es=S)
                    Wr = cx.sb("Wr", [128, 512], es=S); DECr = cx.sb("DECr", [128, 8], es=S)
                    with ExitStack() as S2:
                        lfr = cx.sb("lfr", [128, 8, 64], es=S2); rd8 = cx.sb("rd8", [128, 8], es=S2)
                        bcast_load(rd8, I["ret_decay"][l])
                        cx.act(rd8, rd8, AF.Exp)
                        cx.ts("dve", rd8, rd8, -1.0, ALU.mult)
                        cx.copy("dve", lfr.re("p (h r) d -> p r h d", r=2), rd8.re("p (r h) -> p r h", r=2)[:, :, :, None].bc([128, 2, 4, 64]))
                        for _ in tables(lfr.re("p a d -> p (a d)"), Er, Eir, Wr, None, S2):
                            pass
                        cx.barrier()
                    cx.ts("dve", Eir, Eir, 0.125, ALU.mult)
                    cx.ts("dve", Wr, Wr, 0.125, ALU.mult)
                    xt = [cx.sb("xt", [128, D], es=S) for _ in range(2)]
                    rp64 = [cx.sb("rp64", [128, 128], es=S) for _ in range(2)]
                    rp32 = [cx.sb("rp32", [128, 64], es=S) for _ in range(2)]
                    junk = cx.sb("junk", [128, D], BF16, es=S); xn = cx.sb("xn", [128, D], es=S)
                    ss1 = cx.sb("ss1", [128, 1], es=S); rs1 = cx.sb("rs1", [128, 1], es=S)
                    hT = cx.sb("hT", [128, 8, 128], BF16, es=S)
                    proj = cx.sb("proj", [128, INW], es=S)
                    ssw = cx.sb("ssw", [128, 6], es=S); ssd = cx.sb("ssd", [128, 16], es=S)
                    rsw = cx.sb("rsw", [128, 6], es=S); rsd = cx.sb("rsd", [128, 16], es=S)
                    nw = cx.sb("nw", [128, 6, 64], es=S); nd = cx.sb("nd", [128, 16, 32], es=S)
                    nw2 = cx.sb("nw2", [128, 6, 64], es=S); nd2 = cx.sb("nd2", [128, 16, 32], es=S)
                    rqk = cx.sb("rqk", [128, 8, 64], es=S)
                    r1 = cx.sb("r1", [128, 512], es=S); r2 = cx.sb("r2", [128, 512], es=S)
                    nwb = cx.sb("nwb", [128, 384], BF16, es=S); ndb = cx.sb("ndb", [128, 512], BF16, es=S)
                    tst = cx.sb("tst", [128, 7, 128], BF16, es=S)
                    vbb = cx.sb("vbb", [128, 128], BF16, es=S); vcb = cx.sb("vcb", [128, 256], BF16, es=S)
                    sig = cx.sb("sig", [128, 512], es=S); ff = sig
                    lfh = cx.sb("lfh", [128, 512], es=S); kh = cx.sb("kh", [128, 512], es=S)
                    Eh = cx.sb("Eh", [128, 512], es=S); Eih = cx.sb("Eih", [128, 512], es=S)
                    Wh = cx.sb("Wh", [128, 512], es=S); DECh = cx.sb("DECh", [128, 16], es=S)
                    Qp = cx.sb("Qp", [128, 512], BF16, es=S); Kp = cx.sb("Kp", [128, 512], BF16, es=S)
                    Kpp = cx.sb("Kpp", [128, 512], BF16, es=S)
                    vb16 = cx.sb("vb16", [128, 256], BF16, es=S); gsil = cx.sb("gsil", [128, 256], es=S)
                    gb16 = cx.sb("gb16", [128, 256], BF16, es=S)
                    qkst = cx.sb("qkst", [128, 8, 128], BF16, es=S); ust = cx.sb("ust", [128, 4, 256], BF16, es=S)
                    cumS = ExitStack(); S.enter_context(cumS)
                    _s1 = (cx.sb("Qp1", [128, 512], BF16, es=S), cx.sb("Kp1", [128, 512], BF16, es=S),
                           cx.sb("Kpp1", [128, 512], BF16, es=S), cx.sb("vb161", [128, 256], BF16, es=S),
                           cx.sb("gsil1", [128, 256], es=S), cx.sb("gb161", [128, 256], BF16, es=S),
                           cx.sb("qkst1", [128, 8, 128], BF16, es=S), cx.sb("ust1", [128, 4, 256], BF16, es=S))
                    _s2 = (cx.sb("Qp2", [128, 512], BF16, es=S), cx.sb("Kp2", [128, 512], BF16, es=S),
                           cx.sb("Kpp2", [128, 512], BF16, es=S), cx.sb("vb162", [128, 256], BF16, es=S),
                           _s1[4], cx.sb("gb162", [128, 256], BF16, es=S), _s1[6], _s1[7])
                    _s0 = (Qp, Kp, Kpp, vb16, gsil, gb16, qkst, ust)
                    _s0b = (cx.sb("Qp0b", [128, 512], BF16, es=S), cx.sb("Kp0b", [128, 512], BF16, es=S),
                            cx.sb("Kpp0b", [128, 512], BF16, es=S), cx.sb("vb160b", [128, 256], BF16, es=S),
                            _s0[4], cx.sb("gb160b", [128, 256], BF16, es=S), _s0[6], _s0[7])
                    rp_tmp = {(0, 0): _s0, (0, 1): _s0b, (1, 0): _s1, (1, 1): _s2}
                    DEChs = [DECh, cx.sb("DECh2", [128, 16], es=S)]

                    def xsrc(t):
                        if l == 0:
                            return I["xs"][t * 128:(t + 1) * 128, :] if t < 32 else I["xp"][(t - 32) * 128:(t - 31) * 128, :]
                        return X2[t * 128:(t + 1) * 128, :]

                    r1b = cx.sb("r1b", [128, 512], es=S); r2b = cx.sb("r2b", [128, 512], es=S)

                    def rope(dst, src, tab, H, Dh, r1=r1, r2=r2):
                        hd = Dh // 4
                        Cb = tab[:, None, 0:Dh].bc([128, H, Dh])
                        cx.tt("dve", r1[:, 0:H * Dh].re("p (h d) -> p h d", h=H), src, Cb, ALU.mult)
                        yield
                        s5 = src.re("p h (a s d) -> p h a s d", a=2, s=2)
                        S5 = tab[:, Dh:2 * Dh].re("p (a s d) -> p a s d", a=2, s=2)
                        o5 = r2[:, 0:H * Dh].re("p (h a s d) -> p h a s d", h=H, a=2, s=2)
                        for sidx in range(2):
                            cx.tt("pool", o5[:, :, :, sidx, :], s5[:, :, :, 1 - sidx, :],
                                  S5[:, None, :, sidx, :].bc([128, H, 2, hd]), ALU.mult)
                            yield
                        cx.tt("dve", dst, r1[:, 0:H * Dh].re("p (h d) -> p h d", h=H),
                              r2[:, 0:H * Dh].re("p (h d) -> p h d", h=H), ALU.add)
                        yield

                    def rec_postA(m, t, q, kfb, v, gatesrc, gtab, E, Ei, W):
                        v4 = lambda a: a.re("p (h r d) -> p h r d", h=4, r=2)
                        Qp, Kp, Kpp, vb16, gsil, gb16, qkst, ust = rp_tmp[(m, t % 2)]
                        qb = q.re("p (h d) -> p h d", h=4)[:, :, None, :].bc([128, 4, 2, 64])
                        cx.tt("dve", v4(Qp), v4(E), qb, ALU.mult)
                        yield
                        if kfb.shape[1] == 256:
                            kb = kfb.re("p (h d) -> p h d", h=4)[:, :, None, :].bc([128, 4, 2, 64])
                        else:
                            kb = v4(kfb)
                        cx.tt("pool", v4(Kp), v4(Ei), kb, ALU.mult)
                        yield
                        cx.tt("pool", v4(Kpp), v4(W), kb, ALU.mult)
                        yield
                        cx.copy("act", vb16, v)
                        yield
                        cx.act(gsil, gatesrc, AF.Exp, scale=-1.0)
                        yield
                        cx.act(gsil, gsil, AF.Ln, bias=1.0)
                        yield
                        cx.act(gsil, gsil, AF.Exp, scale=-1.0)
                        yield
                        cx.tt("dve", gsil, gsil, gatesrc, ALU.mult)
                        yield
                        cx.tt("pool", gb16, gsil, gtab, ALU.mult)
                        yield

                    def rec_postB(m, t, DEC):
                        Qp, Kp, Kpp, vb16, gsil, gb16, qkst, ust = rp_tmp[(m, t % 2)]
                        pb = psb(7)
                        for h in range(4):
                            cx.tr(pb[:, h * 128:(h + 1) * 128], Qp[:, h * 128:(h + 1) * 128], identb)
                            cx.tr(pb[:, (4 + h) * 128:(5 + h) * 128], Kp[:, h * 128:(h + 1) * 128], identb)
                        cx.copy("act", qkst.re("p a b -> p (a b)"), pb)
                        yield
                        cx.dma("sp", QKT[m][t], qkst)
                        yield
                        for c in range(4):
                            kwm = {"tile_position": (96, 0)} if c == 3 else {}
                            for h in range(4):
                                cx.mm(ps[4 + c][:, h * 64:(h + 1) * 64], Kpp[c * 32:(c + 1) * 32, h * 128:(h + 1) * 128],
                                      vb16[c * 32:(c + 1) * 32, h * 64:(h + 1) * 64], **kwm)
                        for c in range(4):
                            cx.copy("dve" if c % 2 == 0 else "act", ust[:, c, :], ps[4 + c][:, 0:256])
                        yield
                        cx.dma("sp", UU[m][t], ust.re("p c x -> p (c x)"))
                        yield
                        cx.dma("sp", VV[m][t], vb16)
                        yield
                        cx.dma("sp", GG[m][t], gb16)
                        yield
                        if DEC is not None:
                            cx.dma("sp", DECD[t], DEC)
                            yield

                    def p1_load(t):
                        b = t % 2
                        cx.dma("sp", xt[b], xsrc(t))
                        if t < 32:
                            cx.dma("sp", rp64[b], I["rope64"][t * 128:(t + 1) * 128, :])
                            cx.dma("sp", rp32[b], I["rope32"][t * 128:(t + 1) * 128, :])

                    projs = [proj, cx.sb("projB", [128, INW], es=S)]
                    xns = [xn, xn]
                    hTs = [hT, cx.sb("hTB", [128, 8, 128], BF16, es=S)]
                    ss1s = [ss1, cx.sb("ss1B", [128, 1], es=S)]; rs1s = [rs1, cx.sb("rs1B", [128, 1], es=S)]

                    def front(t):
                        g = 0 if t < 32 else 1
                        x = xt[t % 2]; xn_ = xns[t % 2]; hT_ = hTs[t % 2]; pj = projs[t % 2]
                        cx.act(junk, x, AF.Square, accum_out=ss1s[t % 2])
                        rstd_from_ss(rs1s[t % 2], ss1s[t % 2], D, cumS)
                        cx.ts("dve", xn_, x, rs1s[t % 2], ALU.mult)
                        yield
                        for kc in range(8):
                            cx.tr(ps[kc // 4][:, (kc % 4) * 128:(kc % 4 + 1) * 128], xn_[:, kc * 128:(kc + 1) * 128], ident)
                        for kc in range(8):
                            cx.act(hT_[:, kc, :], ps[kc // 4][:, (kc % 4) * 128:(kc % 4 + 1) * 128], AF.Identity,
                                   scale=A1[:, kc, g:g + 1], bias=SH1[:, kc, g:g + 1])
                        for blk in range(7):
                            pt = ps[2 + blk % 2]
                            for kc in range(8):
                                cx.mm(pt, hT_[:, kc, :], win[:, kc, blk * 512:(blk + 1) * 512], start=(kc == 0), stop=(kc == 7))
                            cx.copy("act" if blk % 2 == 0 else "dve", pj[:, blk * 512:(blk + 1) * 512], pt)
                            yield

                    p1_load(0)
                    for _ in front(0):
                        pass
                    fgen = [None]

                    def adv(n=1):
                        for _ in range(n):
                            if fgen[0] is not None:
                                next(fgen[0], None)

                    nwbs = [nwb, cx.sb("nwb2", [128, 384], BF16, es=S)]; ndbs = [ndb, cx.sb("ndb2", [128, 512], BF16, es=S)]

                    def chain_bc(t):
                        g = 0 if t < 32 else 1
                        latent = t < 32
                        proj = projs[t % 2]
                        pw = proj[:, C_WQ:C_WQ + 384].re("p (h d) -> p h d", h=6)
                        pd = proj[:, C_DQ:C_DQ + 512].re("p (h d) -> p h d", h=16)
                        cx.tt("pool", nw2, pw, pw, ALU.mult)
                        yield
                        cx.tt("pool", nd2, pd, pd, ALU.mult)
                        yield
                        cx.reduce("dve", ssw, nw2, ALU.add)
                        yield
                        cx.reduce("dve", ssd, nd2, ALU.add)
                        yield
                        rstd_from_ss(rsw, ssw, 64, cumS)
                        rstd_from_ss(rsd, ssd, 32, cumS)
                        cx.tt("dve", nw, pw, rsw[:, :, None].bc([128, 6, 64]), ALU.mult)
                        yield
                        cx.tt("pool", nw, nw, gw, ALU.mult)
                        yield
                        cx.tt("dve", nd, pd, rsd[:, :, None].bc([128, 16, 32]), ALU.mult)
                        yield
                        cx.tt("pool", nd, nd, gd, ALU.mult)
                        yield
                        if not latent:
                            sq = 1 + (t - 32) // 2; bl = sq - 1; lt = (t - 32) % 2
                            tsl = slice(lt * 128, (lt + 1) * 128)
                            cx.dma("sp", O["o_wk"][bl, l, :, tsl, :].re("k t d -> t k d"), nw[:, 4:6, :])
                            yield
                            cx.dma("sp", O["o_wv"][bl, l, :, tsl, :].re("k t d -> t k d"),
                                   proj[:, C_WV:C_WV + 128].re("p (k d) -> p k d", k=2))
                            yield
                            cx.dma("sp", O["o_dk"][bl, l, :, :, tsl, :].re("h c t d -> t (h c) d"), nd[:, 8:16, :])
                            yield
                            cx.dma("sp", O["o_dv"][bl, l, :, tsl, :].re("h t d -> t h d"),
                                   proj[:, C_DV:C_DV + 256].re("p (h d) -> p h d", h=4))
                            yield
                            nwr, ndr = nw, nd
                        else:
                            yield from rope(nw2, nw, rp64[t % 2], 6, 64)
                            yield from rope(nd2, nd, rp32[t % 2], 16, 32)
                            nwr, ndr = nw2, nd2
                        nwb_ = nwbs[t % 2]; ndb_ = ndbs[t % 2]
                        tk = slice(t * 128, (t + 1) * 128)
                        cx.copy("act", vbb, proj[:, C_WV:C_WV + 128])
                        yield
                        cx.copy("act", vcb, proj[:, C_DV:C_DV + 256])
                        yield
                        cx.dma("sp", VB[tk, :], vbb)
                        yield
                        cx.dma("sp", VC[tk, :], vcb)
                        yield
                        cx.copy("pool", nwb_[:, 0:256].re("p (g k d) -> p k g d", g=2, k=2), nwr[:, 0:4, :].re("p (k g) d -> p k g d", k=2))
                        yield
                        cx.copy("act", nwb_[:, 256:384], nwr[:, 4:6, :].re("p h d -> p (h d)"))
                        yield
                        cx.copy("act", ndb_, ndr.re("p h d -> p (h d)"))
                        yield

                    def chain_bcB(t):
                        nwb_ = nwbs[t % 2]; ndb_ = ndbs[t % 2]
                        tk = slice(t * 128, (t + 1) * 128)
                        pb = psb(4)
                        for gq in range(2):
                            cx.tr(pb[:, gq * 128:(gq + 1) * 128], nwb_[:, gq * 128:(gq + 1) * 128], identb)
                        cx.tr(pb[:, 256:384], nwb_[:, 256:384], identb)
                        for j in range(4):
                            cx.tr(pb[:, 384 + j * 128:512 + j * 128], ndb_[:, j * 128:(j + 1) * 128], identb)
                        cx.copy("act", tst.re("p a b -> p (a b)"), pb[:, 0:896])
                        yield
                        cx.dma("sp", QTB[:, t // 2, :, (t % 2) * 128:(t % 2 + 1) * 128], tst[:, 0:2, :])
                        yield
                        cx.dma("sp", KTB[:, tk], tst[:, 2, :])
                        yield
                        cx.dma("sp", QTC[:, :, tk], tst[:, 3:5, :])
                        yield
                        cx.dma("sp", KTC[:, :, tk], tst[:, 5:7, :])
                        yield

                    def chain_ret(t):
                        g = 0 if t < 32 else 1
                        latent = t < 32
                        proj = projs[t % 2]
                        if latent:
                            yield from rope(rqk, proj[:, 0:512].re("p (h d) -> p h d", h=8), rp64[t % 2], 8, 64, r1b, r2b)
                            rqkr = rqk
                        else:
                            rqkr = proj[:, 0:512].re("p (h d) -> p h d", h=8)
                        yield from rec_postA(0, t, rqkr[:, 0:4, :].re("p h d -> p (h d)"), rqkr[:, 4:8, :].re("p h d -> p (h d)"), proj[:, C_RV:C_RV + 256],
                                 proj[:, C_RG:C_RG + 256], gret, Er, Eir, Wr)
                        yield

                    def chain_hg(t):
                        g = 0 if t < 32 else 1
                        latent = t < 32
                        proj = projs[t % 2]
                        cx.act(sig.re("p (h r d) -> p r h d", h=4, r=2), proj[:, C_HZ:C_HZ + 512].re("p (r h d) -> p r h d", r=2, h=4), AF.Exp, scale=-1.0)
                        yield
                        cx.act(sig, sig, AF.Ln, bias=1.0)
                        yield
                        cx.act(sig, sig, AF.Exp, scale=-1.0)
                        yield
                        cx.tt("dve", ff, sig, OMLB, ALU.mult)
                        yield
                        cx.tt("dve", ff, ff, LB, ALU.add)
                        yield
                        cx.ts("dve", ff, ff, 1e-30, ALU.max)
                        yield
                        cx.act(lfh, ff, AF.Ln)
                        yield
                        cx.ts("pool", kh, ff, -1.0, ALU.mult, 1.0, ALU.add)
                        yield
                        yield from tables(lfh, Eh, Eih, Wh, DEChs[t % 2], cumS)
                        yield from rec_postA(1, t, proj[:, C_HQ:C_HQ + 256], kh, proj[:, C_HI:C_HI + 256],
                                 proj[:, C_HG:C_HG + 256], ghg, Eh, Eih, Wh)
                        yield

                    for t in range(NT):
                        chains = []
                        if t + 1 < NT:
                            p1_load(t + 1)
                            chains.append(cx.record(front(t + 1)))
                        chains += [cx.record(chain_hg(t)), cx.record(chain_bc(t)), cx.record(chain_ret(t))]
                        if t >= 1:
                            chains.append(cx.record(rec_postB(1, t - 1, DEChs[(t - 1) % 2])))
                            chains.append(cx.record(chain_bcB(t - 1)))
                            chains.append(cx.record(rec_postB(0, t - 1, None)))
                        cx.schedule(chains)
                    for _ in rec_postB(1, NT - 1, DEChs[(NT - 1) % 2]):
                        pass
                    for _ in chain_bcB(NT - 1):
                        pass
                    for _ in rec_postB(0, NT - 1, None):
                        pass
                cx.barrier()
                chk(1)

                for m in (0, 1):
                    with ExitStack() as S:
                        Sbf = cx.sb("Sbf", [128, 4 * NT, 256], BF16, es=S)
                        SbfF = V(Buf("SbfF"), Sbf.ap); SbfB = V(Buf("SbfB"), Sbf.ap)
                        decr = None
                        if m == 0:
                            decr = cx.sb("decr", [128, 8], es=S)
                            bcast_dummy = None
                            rdT = cx.sb("rdT", [128, 4], es=S)
                            cx.dma("sp", rdT[0:64, :], V(I["ret_decay"].buf, I["ret_decay"].ap[l, 0:4].partition_broadcast(64)))
                            cx.dma("sp", rdT[64:128, :], V(I["ret_decay"].buf, I["ret_decay"].ap[l, 4:8].partition_broadcast(64)))
                            cx.act(rdT, rdT, AF.Exp)
                            cx.act(decr[:, 0:4], rdT, AF.Exp, scale=-32.0)
                        for (t0, ntl, g, latent) in SEQS:
                            nch = 4 * ntl
                            with ExitStack() as S2:
                                Ua = cx.sb("Ua", [128, ntl, 1024], BF16, es=S2)
                                cx.dma("sp", Ua, UU[m][t0:t0 + ntl].re("t p x -> p t x"))
                                Ua4 = Ua.re("p t (c x) -> p (t c) x", c=4)
                                Da = None
                                if m == 1:
                                    Da = cx.sb("Da", [128, ntl, 16], es=S2)
                                    cx.dma("sp", Da, DECD[t0:t0 + ntl].re("t p x -> p t x"))
                                St = cx.sb("St", [128, 256], es=S2); Tm = cx.sb("Tm", [128, 256], es=S2)
                                StF = V(Buf("StF"), St.ap[0:64]); StB = V(Buf("StB"), St.ap[64:128])
                                TmF = V(Buf("TmF"), Tm.ap[0:64]); TmB = V(Buf("TmB"), Tm.ap[64:128])
                                src_state = I["sret"] if m == 0 else I["shg"]
                                if latent:
                                    for r in range(2):
                                        cx.dma("sp", (StF if r == 0 else StB).re("p (h v) -> p h v", h=4),
                                               src_state[l, r].re("h d v -> d h v"))
                                else:
                                    cx.memset("dve", StF, 0.0); cx.memset("pool", StB, 0.0)
                                for s in range(nch):
                                    for r, (eng, Sx, Tx, Sb) in enumerate((("dve", StF, TmF, SbfF), ("pool", StB, TmB, SbfB))):
                                        n = s if r == 0 else nch - 1 - s
                                        prt = slice(r * 64, (r + 1) * 64)
                                        gch = 4 * t0 + n
                                        cx.copy("act", Sb[prt, gch, :], Sx)
                                        if m == 0:
                                            dv_ = decr[prt, 0:4, None].bc([64, 4, 64])
                                        else:
                                            tt_, cc_ = n // 4, n % 4
                                            dv_ = Da[prt, tt_, :].re("p (h c) -> p h c", c=4)[:, :, cc_:cc_ + 1].bc([64, 4, 64])
                                        cx.tt(eng, Tx.re("p (h v) -> p h v", h=4), Sx.re("p (h v) -> p h v", h=4), dv_, ALU.mult)
                                        cx.tt(eng, Sx, Tx, Ua4[prt, n, :], ALU.add)
                                if not latent:
                                    bl = (t0 - 32) // 2
                                    dst = O["o_sret"] if m == 0 else O["o_shg"]
                                    for r in range(2):
                                        cx.dma("sp", dst[bl, l, r].re("h d v -> d h v"),
                                               (StF if r == 0 else StB).re("p (h v) -> p h v", h=4))
                            cx.barrier()
                        qk = [cx.sb("qk", [128, 8, 128], BF16, es=S) for _ in range(4)]
                        vv = [cx.sb("vv", [128, 256], BF16, es=S) for _ in range(4)]
                        gg = [cx.sb("gg", [128, 256], BF16, es=S) for _ in range(4)]
                        Afs = [cx.sb("Af", [128, 512], BF16, es=S) for _ in range(2)]; Ab = cx.sb("Ab", [128, 512], BF16, es=S)
                        sq_ = cx.sb("sq", [128, 256], es=S); ss4 = cx.sb("ss4", [128, 4], es=S); rs4 = cx.sb("rs4", [128, 4], es=S)
                        on_ = cx.sb("on", [128, 256], es=S); of_ = [cx.sb("of", [128, 256], BF16, es=S) for _ in range(2)]
                        cumS = ExitStack(); S.enter_context(cumS)

                        def p2_load(t):
                            b = t % 4
                            cx.dma("sp", qk[b], QKT[m][t]); cx.dma("sp", vv[b], VV[m][t]); cx.dma("sp", gg[b], GG[m][t])
                        def p2_A(t):
                            b = t % 2
                            Q = qk[t % 4][:, 0:4, :]; K = qk[t % 4][:, 4:8, :]
                            Af = Afs[b]
                            pf = ps[0 + 2 * b]; pbk = ps[1 + 2 * b]
                            for h in range(4):
                                cx.mm(pf[:, h * 128:(h + 1) * 128], K[0:64, h, :], Q[0:64, h, :])
                                cx.mm(pbk[:, h * 128:(h + 1) * 128], K[64:128, h, :], Q[64:128, h, :])
                            cx.tt("dve", Af.re("p (h i) -> p h i", h=4), pf.re("p (h i) -> p h i", h=4),
                                  tmaskb[:, 0:1, :].bc([128, 4, 128]), ALU.mult)
                            cx.tt("dve", Ab.re("p (h i) -> p h i", h=4), pbk.re("p (h i) -> p h i", h=4),
                                  tmaskb[:, 1:2, :].bc([128, 4, 128]), ALU.mult)
                            cx.tt("pool", Af, Af, Ab, ALU.add)

                        def p2_B2(t):
                            b = t % 2
                            po = ps[4 + t % 2]
                            cx.act(sq_, po[:, 0:256], AF.Square)
                            cx.reduce("dve", ss4, sq_.re("p (h d) -> p h d", h=4), ALU.add)
                            rstd_from_ss(rs4, ss4, 64, cumS)
                            cx.tt("dve", on_.re("p (h d) -> p h d", h=4), po[:, 0:256].re("p (h d) -> p h d", h=4),
                                  rs4[:, :, None].bc([128, 4, 64]), ALU.mult)
                            cx.tt("pool", of_[b], on_, gg[t % 4], ALU.mult)
                            co = 0 if m == 0 else 768
                            cx.dma("sp", OCAT[t * 128:(t + 1) * 128, co:co + 256], of_[b])

                        p2_load(0)
                        if NT > 1:
                            p2_load(1)
                        p2_A(0)
                        for t in range(NT):
                            b = t % 2
                            if t + 2 < NT:
                                p2_load(t + 2)
                            if t + 1 < NT:
                                p2_A(t + 1)
                            Q = qk[t % 4][:, 0:4, :]; K = qk[t % 4][:, 4:8, :]
                            Af = Afs[b]
                            po = ps[4 + t % 2]
                            for h in range(4):
                                cx.mm(po[:, h * 64:(h + 1) * 64], Af[:, h * 128:(h + 1) * 128], vv[t % 4][:, h * 64:(h + 1) * 64],
                                      start=True, stop=False, skip_group_check=True)
                                for c in range(4):
                                    kwm = {"tile_position": (0, 96)} if c == 3 else {}
                                    cx.mm(po[c * 32:(c + 1) * 32, h * 64:(h + 1) * 64], Q[:, h, c * 32:(c + 1) * 32],
                                          Sbf[:, 4 * t + c, h * 64:(h + 1) * 64], start=False, stop=(c == 3),
                                          extra_reads=(SbfF, SbfB), skip_group_check=True, **kwm)
                            if t >= 1:
                                p2_B2(t - 1)
                        p2_B2(NT - 1)
                    cx.barrier()

                chk(2)
                for (t0, ntl, g, latent) in [(0, 32, 0, True), (32, 2 * NPS, 1, False)]:
                    L = ntl * 128
                    nkt = ntl + (4 if latent else 0)
                    tok0 = t0 * 128
                    with ExitStack() as S:
                        qtb = cx.sb("qtb", [128, 2, L // 256, 512], BF16, es=S); ktb = cx.sb("ktb", [128, nkt * 128], BF16, es=S)
                        vbt = cx.sb("vbt", [128, nkt, 2, 65], BF16, es=S)
                        cx.memset("pool", qtb, 0.0)
                        for kvq in range(2):
                            cx.dma("sp", qtb[kvq * 64:(kvq + 1) * 64, kvq], QTB[kvq * 64:(kvq + 1) * 64, tok0 // 256:(tok0 + L) // 256].re("p b g q -> p b (g q)"))
                        cx.dma("sp", ktb[:, 0:L], KTB[:, tok0:tok0 + L])
                        cx.memset("pool", vbt[:, :, :, 64:65], 1.0)
                        for kt in range(ntl):
                            cx.dma("sp", vbt[:, kt, :, 0:64], VB[tok0 + kt * 128:tok0 + (kt + 1) * 128, :].re("p (k d) -> p k d", k=2))
                        if latent:
                            ck = cx.sb("ck", [128, 4, 2, 64], es=S)
                            for kt in range(4):
                                cx.dma("sp", ck[:, kt], I["cwk"][l, :, kt * 128:(kt + 1) * 128, :].re("k p d -> p k d"))
                                cx.dma("pool", vbt[:, ntl + kt, :, 0:64], I["cwv"][l, :, kt * 128:(kt + 1) * 128, :].re("k p d -> p k d"))
                            for kt in range(4):
                                cx.tr(ps[0][:, kt * 128:(kt + 1) * 128], ck[:, kt].re("p k d -> p (k d)"), ident)
                            cx.copy("act", ktb[:, L:L + 512], ps[0])
                        esk = cx.sb("esk", [128, 4], es=S); eskv = cx.sb("eskv", [128, 2, 2, 2], es=S)
                        bcast_load(esk, I["win_sink"][l])
                        cx.act(esk, esk, AF.Exp)
                        cx.copy("dve", eskv, esk.re("p (k g) -> p k g", k=2)[:, :, :, None].bc([128, 2, 2, 2]))
                        Pt = [cx.sb("Pt", [128, 512], BF16, es=S) for _ in range(4)]
                        den = cx.sb("den", [128, 4], es=S); ob = [cx.sb("ob", [128, 2, 256], BF16, es=S) for _ in range(2)]
                        pi = 0
                        for qb in range(L // 256):
                            a = 2 * qb
                            if latent:
                                keys = [(a + o, o + 1) for o in (-1, 0, 1, 2) if 0 <= a + o < ntl] + [(ntl + k, None) for k in range(4)]
                            else:
                                keys = [(2 * qb, None), (2 * qb + 1, None)]
                            obq = ob[qb % 2]
                            its = [(kv, ki, kt, mo) for kv in range(2) for ki, (kt, mo) in enumerate(keys)]

                            def b_score(i):
                                kv, ki, kt, mo = its[i]
                                cx.mm(ps[i % 3], ktb[:, kt * 128:(kt + 1) * 128], qtb[:, kv, qb, :])

                            b_score(0)
                            if len(its) > 1:
                                b_score(1)
                            for i, (kv, ki, kt, mo) in enumerate(its):
                                po = ps[4 + kv + 2 * (qb % 2)]
                                if i + 2 < len(its):
                                    b_score(i + 2)
                                P = Pt[pi % 4]; pi += 1
                                cx.act(P, ps[i % 3], AF.Exp, scale=0.125)
                                if mo is not None:
                                    cx.tt("dve", P, P, wmaskb[:, mo, :], ALU.mult)
                                for gq in range(2):
                                    for u in range(2):
                                        j = gq * 2 + u
                                        cx.mm(po[:, j * 65:(j + 1) * 65], P[:, gq * 256 + u * 128:gq * 256 + (u + 1) * 128],
                                              vbt[:, kt, kv, :], start=(ki == 0 and j == 0), stop=(ki == len(keys) - 1),
                                              skip_group_check=True)
                                if ki == len(keys) - 1:
                                    cx.tt("dve", den, po[:, 64:260:65], eskv[:, kv].re("p g u -> p (g u)"), ALU.add)
                                    cx.recip(den, den)
                                    o4 = obq.re("p u (k g d) -> p k g u d", k=2, g=2)[:, kv]
                                    cx.tt("dve", o4, po[:, 0:260].re("p (g u e) -> p g u e", g=2, u=2)[:, :, :, 0:64],
                                          den.re("p (g u) -> p g u", g=2)[:, :, :, None].bc([128, 2, 2, 64]), ALU.mult)
                            cx.dma("sp", OCAT[tok0 + qb * 256:tok0 + (qb + 1) * 256, 256:512].re("(u p) c -> p u c", p=128), obq)
                    cx.barrier()
                    with ExitStack() as S:
                        qtc = cx.sb("qtc", [128, 4, 2, L], BF16, es=S); ktc = cx.sb("ktc", [128, 2, nkt * 128], BF16, es=S)
                        vct = cx.sb("vct", [128, nkt, 4, 65], BF16, es=S)
                        cx.memset("pool", qtc, 0.0)
                        for vq in range(4):
                            cx.dma("sp", qtc[vq * 32:(vq + 1) * 32, vq, :, :], QTC[vq * 32:(vq + 1) * 32, :, tok0:tok0 + L])
                        cx.dma("sp", ktc[:, :, 0:L], KTC[:, :, tok0:tok0 + L])
                        cx.memset("pool", vct[:, :, :, 64:65], 1.0)
                        for kt in range(ntl):
                            cx.dma("sp", vct[:, kt, :, 0:64], VC[tok0 + kt * 128:tok0 + (kt + 1) * 128, :].re("p (h d) -> p h d", h=4))
                        if latent:
                            ck = cx.sb("ckc", [128, 4, 8, 32], es=S)
                            for kt in range(4):
                                cx.dma("sp", ck[:, kt], I["cdk"][l, :, :, kt * 128:(kt + 1) * 128, :].re("h c p d -> p (h c) d"))
                                cx.dma("pool", vct[:, ntl + kt, :, 0:64], I["cdv"][l, :, kt * 128:(kt + 1) * 128, :].re("h p d -> p h d"))
                            for hp in range(2):
                                for kt in range(4):
                                    cx.tr(ps[hp][:, kt * 128:(kt + 1) * 128], ck[:, kt, hp * 4:(hp + 1) * 4, :].re("p a d -> p (a d)"), ident)
                                cx.copy("act", ktc[:, hp, L:L + 512], ps[hp])
                        dl = cx.sb("dl", [128, 128], es=S); dp = cx.sb("dp", [128, 2, 32], es=S); d2 = cx.sb("d2", [128, 2], es=S)
                        lam = cx.sb("lam", [128, 1], es=S)
                        bcast_load(dl, I["diff_lambda"][l])
                        dl4 = dl.re("p (a b d) -> p a b d", a=2, b=2)
                        cx.tt("dve", dp, dl4[:, :, 0, :], dl4[:, :, 1, :], ALU.mult)
                        cx.reduce("dve", d2, dp, ALU.add)
                        cx.act(d2, d2, AF.Exp)
                        cx.tt("dve", lam, d2[:, 0:1], d2[:, 1:2], ALU.subtract)
                        cx.ts("dve", lam, lam, float(lam_init), ALU.add)
                        gdn = cx.sb("gdn", [128, 256], es=S)
                        bcast_load(gdn, I["diff_norm_g"][l])
                        cx.ts("dve", gdn, gdn, float(1.0 - lam_init), ALU.mult)
                        QB = 512 if latent else 256; nsub = QB // 128
                        Pt = [cx.sb("Ptc", [128, 512], BF16, es=S) for _ in range(4)]
                        dn1 = cx.sb("dn1", [128, 4], es=S); dn2 = cx.sb("dn2", [128, 4], es=S)
                        t1_ = cx.sb("t1c", [128, 4, 64], es=S); t2_ = cx.sb("t2c", [128, 4, 64], es=S)
                        od = cx.sb("od", [128, 4, 4, 64], es=S)
                        sqd = cx.sb("sqd", [128, 4, 4, 64], es=S); ssq = cx.sb("ssq", [128, 16], es=S); rsq = cx.sb("rsq", [128, 16], es=S)
                        odb = [cx.sb("odb", [128, 4, 256], BF16, es=S) for _ in range(2)]
                        cumS = ExitStack(); S.enter_context(cumS)
                        pi = 0
                        for qb in range(L // QB):
                            kts = list(range(nkt)) if latent else [2 * qb, 2 * qb + 1]
                            its = [(h, c, kt) for h in range(4) for c in range(2) for kt in kts]

                            def c_score(i):
                                h, c, kt = its[i]
                                hp, hh = h // 2, h % 2
                                cx.mm(ps[i % 3][:, 0:QB], ktc[:, hp, kt * 128:(kt + 1) * 128], qtc[:, hh * 2 + c, hp, qb * QB:(qb + 1) * QB])

                            c_score(0)
                            c_score(1)
                            for i, (h, c, kt) in enumerate(its):
                                pA = ps[4 + 2 * (h % 2)]; pB = ps[5 + 2 * (h % 2)]
                                po = pA if c == 0 else pB
                                if i + 2 < len(its):
                                    c_score(i + 2)
                                P = Pt[pi % 4]; pi += 1
                                cx.act(P[:, 0:QB], ps[i % 3][:, 0:QB], AF.Exp, scale=32 ** -0.5)
                                for u in range(nsub):
                                    cx.mm(po[:, u * 65:(u + 1) * 65], P[:, u * 128:(u + 1) * 128], vct[:, kt, h, :],
                                          start=(kt == kts[0] and u == 0), stop=(kt == kts[-1]), skip_group_check=True)
                                if c == 1 and kt == kts[-1]:
                                    e1 = nsub * 65
                                    cx.recip(dn1[:, 0:nsub], pA[:, 64:e1:65])
                                    cx.recip(dn2[:, 0:nsub], pB[:, 64:e1:65])
                                    cx.ts("dve", dn2[:, 0:nsub], dn2[:, 0:nsub], lam[:, 0:1], ALU.mult)
                                    cx.tt("dve", t1_[:, 0:nsub, :], pA[:, 0:e1].re("p (u e) -> p u e", u=nsub)[:, :, 0:64],
                                          dn1[:, 0:nsub, None].bc([128, nsub, 64]), ALU.mult)
                                    cx.tt("dve", t2_[:, 0:nsub, :], pB[:, 0:e1].re("p (u e) -> p u e", u=nsub)[:, :, 0:64],
                                          dn2[:, 0:nsub, None].bc([128, nsub, 64]), ALU.mult)
                                    cx.tt("pool", od[:, 0:nsub, h, :], t1_[:, 0:nsub, :], t2_[:, 0:nsub, :], ALU.subtract)
                            nn = nsub * 4
                            odv = od[:, 0:nsub].re("p u h d -> p (u h) d")
                            sqv = sqd[:, 0:nsub].re("p u h d -> p (u h) d")
                            cx.tt("pool", sqv, odv, odv, ALU.mult)
                            cx.reduce("dve", ssq[:, 0:nn], sqv, ALU.add)
                            rstd_from_ss(rsq[:, 0:nn], ssq[:, 0:nn], 64, cumS)
                            cx.tt("dve", sqv, odv, rsq[:, 0:nn, None].bc([128, nn, 64]), ALU.mult)
                            ob_ = odb[qb % 2]
                            cx.tt("pool", ob_[:, 0:nsub, :], sqd[:, 0:nsub].re("p u h d -> p u (h d)"),
                                  gdn[:, None, :].bc([128, nsub, 256]), ALU.mult)
                            cx.dma("sp", OCAT[tok0 + qb * QB:tok0 + (qb + 1) * QB, 512:768].re("(u p) c -> p u c", p=128), ob_[:, 0:nsub, :])
                    cx.barrier()

                chk(3)
                with ExitStack() as SF:
                    wfi = cx.sb("wfi", [128, 8, 2 * FH], BF16, es=SF)
                    with ExitStack() as S:
                        wout = cx.sb("wout", [128, 8, D], BF16, es=S)
                        for blk in range(2):
                            cx.dma("pool", wout[:, :, blk * 512:(blk + 1) * 512],
                                   I["w_out"][l, :, blk * 512:(blk + 1) * 512].re("(kc p) n -> p kc n", p=128))
                        for blk in range(11):
                            cx.dma("pool", wfi[:, :, blk * 512:(blk + 1) * 512],
                                   I["w_ffn_in"][l, :, blk * 512:(blk + 1) * 512].re("(kc p) n -> p kc n", p=128))
                        oc = [cx.sb("oc", [128, D], BF16, es=S) for _ in range(3)]
                        xt = [cx.sb("xt3", [128, D], es=S) for _ in range(3)]
                        ocTs = [cx.sb("ocT", [128, 8, 128], BF16, es=S) for _ in range(2)]
                        tmps = [cx.sb("tmp3", [128, D], es=S) for _ in range(2)]; xo = [cx.sb("xo3", [128, D], es=S) for _ in range(2)]

                        def p3_load(t):
                            cx.dma("sp", oc[t % 3], OCAT[t * 128:(t + 1) * 128, :])
                            cx.dma("sp", xt[t % 3], xsrc(t))

                        def p3_A(t):
                            pb = psb(t % 2)
                            for kc in range(8):
                                cx.tr(pb[:, kc * 128:(kc + 1) * 128], oc[t % 3][:, kc * 128:(kc + 1) * 128], identb)
                            cx.copy("act", ocTs[t % 2].re("p a b -> p (a b)"), pb)

                        p3_load(0)
                        if NT > 1:
                            p3_load(1)
                        p3_A(0)
                        for t in range(NT):
                            g = 0 if t < 32 else 1
                            if t + 2 < NT:
                                p3_load(t + 2)
                            if t + 1 < NT:
                                p3_A(t + 1)
                            ocT = ocTs[t % 2]; tmp = tmps[t % 2]
                            for nb in range(2):
                                pt = ps[2 + nb + 2 * (t % 2)]
                                for kc in range(8):
                                    cx.mm(pt, ocT[:, kc, :], wout[:, kc, nb * 512:(nb + 1) * 512], start=(kc == 0), stop=(kc == 7))
                                cx.tt("dve", tmp[:, nb * 512:(nb + 1) * 512], pt, G1bc[:, g, nb * 512:(nb + 1) * 512], ALU.mult)
                            cx.tt("pool", xo[t % 2], tmp, xt[t % 3], ALU.add)
                            cx.dma("sp", X1[t * 128:(t + 1) * 128, :], xo[t % 2])
                    cx.barrier()

                    chk(4)
                    with ExitStack() as S:
                        wfo = cx.sb("wfo", [128, 22, D], BF16, es=S)
                        for blk in range(2):
                            cx.dma("pool", wfo[:, :, blk * 512:(blk + 1) * 512],
                                   I["w_ffn_out"][l, :, blk * 512:(blk + 1) * 512].re("(j p) n -> p j n", p=128))
                        TB = 2
                        xb = [cx.sb("xb", [128, TB, D], es=S) for _ in range(2)]
                        junk = cx.sb("junk4", [128, D], BF16, es=S); xn = cx.sb("xn4", [128, D], es=S)
                        ss1 = cx.sb("ss14", [128, 1], es=S); rs1 = cx.sb("rs14", [128, 1], es=S)
                        hT2s = [cx.sb("hT2", [128, 8, TB * 128], BF16, es=S) for _ in range(2)]
                        hidt = cx.sb("hid", [128, 22, TB * 128], BF16, es=S)
                        hid = [V(Buf(f"hid{j}"), hidt.ap[:, j, :]) for j in range(22)]
                        sg = [cx.sb("sg", [128, TB * 128], BF16, es=S) for _ in range(2)]
                        tmp = cx.sb("tmp4", [128, D], es=S)
                        cumS = ExitStack(); S.enter_context(cumS)
                        NB = NT // TB
                        ydst = None

                        def p4_load(bi):
                            cx.dma("sp", xb[bi % 2], X1[bi * TB * 128:(bi + 1) * TB * 128, :].re("(u p) c -> p u c", p=128))
                        def p4_front(bi):
                            g = 0 if bi * TB < 32 else 1
                            xbb = xb[bi % 2]; hT2 = hT2s[bi % 2]
                            for u in range(TB):
                                cx.act(junk, xbb[:, u, :], AF.Square, accum_out=ss1)
                                rstd_from_ss(rs1, ss1, D, cumS)
                                cx.ts("dve", xn, xbb[:, u, :], rs1, ALU.mult)
                                for kc in range(8):
                                    cx.tr(ps[kc // 4][:, (kc % 4) * 128:(kc % 4 + 1) * 128], xn[:, kc * 128:(kc + 1) * 128], ident)
                                for kc in range(8):
                                    cx.act(hT2[:, kc, u * 128:(u + 1) * 128], ps[kc // 4][:, (kc % 4) * 128:(kc % 4 + 1) * 128], AF.Identity,
                                           scale=A2[:, kc, g:g + 1], bias=SH2[:, kc, g:g + 1])

                        p4_load(0)
                        p4_front(0)
                        for bi in range(NB):
                            t0 = bi * TB
                            g = 0 if t0 < 32 else 1
                            if bi + 1 < NB:
                                p4_load(bi + 1)
                            xbb = xb[bi % 2]; hT2 = hT2s[bi % 2]
                            for j in range(22):
                                if j == 8 and bi + 1 < NB:
                                    p4_front(bi + 1)
                                pg = ps[2 + (j % 2)]; pu = ps[4 + (j % 2)]
                                for kc in range(8):
                                    cx.mm(pg[:, 0:TB * 128], wfi[:, kc, j * 128:(j + 1) * 128], hT2[:, kc, :], start=(kc == 0), stop=(kc == 7))
                                for kc in range(8):
                                    cx.mm(pu[:, 0:TB * 128], wfi[:, kc, FH + j * 128:FH + (j + 1) * 128], hT2[:, kc, :], start=(kc == 0), stop=(kc == 7))
                                cx.act(sg[j % 2], pg[:, 0:TB * 128], AF.Silu)
                                cx.tt("dve", hid[j], sg[j % 2], pu[:, 0:TB * 128], ALU.mult)
                            for u in range(TB):
                                t = t0 + u
                                for nb in range(2):
                                    pt = ps[6 + nb]
                                    for j in range(22):
                                        cx.mm(pt, hid[j][:, u * 128:(u + 1) * 128], wfo[:, j, nb * 512:(nb + 1) * 512], start=(j == 0), stop=(j == 21))
                                    cx.tt("dve", tmp[:, nb * 512:(nb + 1) * 512], pt, G2bc[:, g, nb * 512:(nb + 1) * 512], ALU.mult)
                                cx.tt("pool", xbb[:, u, :], tmp, xbb[:, u, :], ALU.add)
                                if last:
                                    dst = O["ys"][t * 128:(t + 1) * 128, :] if t < 32 else O["yp"][(t - 32) * 128:(t - 31) * 128, :]
                                else:
                                    dst = X2[t * 128:(t + 1) * 128, :]
                                cx.dma("sp", dst, xbb[:, u, :])
                    cx.barrier()
          except _Stop:
            cx.barrier()
            break
        cx.finish()
    try:
        es.close()
    except Exception as e:
        if stage == 99:
            raise
    print("n_inst", cx.n_inst)
    return nc


def rope_tables():
    def tab(Dh):
        dax = Dh // 2; half = dax // 2
        inv = 10000.0 ** (-np.arange(half, dtype=np.float32) / half)
        t = np.arange(LS)
        row = (t // 64).astype(np.float32); col = (t % 64).astype(np.float32)
        out = np.zeros((LS, 2 * Dh), np.float32)
        for ai, pos in enumerate((row, col)):
            ang = (pos[:, None] * inv[None, :]).astype(np.float32)
            c, s = np.cos(ang), np.sin(ang)
            b = ai * dax
            out[:, b:b + half] = c; out[:, b + half:b + dax] = c
            out[:, Dh + b:Dh + b + half] = -s; out[:, Dh + b + half:Dh + b + dax] = s
        return out
    return tab(64), tab(32)


def const_inputs():
    r64, r32 = rope_tables()
    j = np.arange(128)[:, None]; i = np.arange(128)[None, :]
    same = (j // 32) == (i // 32)
    tm = np.stack([(j <= i) & same, (j >= i) & same, same], axis=1).astype(np.float32)
    ci = np.zeros((128, 4), np.float32)
    for c in range(4):
        ci[c * 32:(c + 1) * 32, c] = 1
    MU = (j >= i).astype(np.float32); ML = (j <= i).astype(np.float32); ON = np.ones((128, 128), np.float32); Z = np.zeros((128, 128), np.float32)
    wm = np.zeros((128, 4, 2, 2, 128), np.float32)
    for mo in range(4):
        o = mo - 1
        for u in range(2):
            rel = o - u
            blk = {-2: Z, -1: MU, 0: ON, 1: ML, 2: Z}[rel]
            wm[:, mo, :, u, :] = blk[:, None, :]
    return {"ident": np.eye(128, dtype=np.float32), "rope64": r64, "rope32": r32, "tmask": tm, "chunkind": ci,
            "wmask": wm.reshape(128, 4, 512)}


_NC_CACHE = {}


def kernel(**inputs):
    debug = bool(inputs.pop("_debug", False))
    nlayers = int(inputs.pop("_nlayers", DEPTH))
    stage = int(inputs.pop("_stage", 99))
    f = lambda a: np.ascontiguousarray(np.asarray(a, dtype=np.float32))
    X = {k: f(v) for k, v in inputs.items()}
    consts = const_inputs()
    key = (debug, nlayers, stage)
    if key not in _NC_CACHE:
        _NC_CACHE[key] = build(debug, nlayers, stage)
    nc = _NC_CACHE[key]
    in_maps = []
    for core in range(8):
        b = core % 4
        m = {
            "xs": X["x_sample"][b], "xp": X["x_prompt"][core * 4:(core + 1) * 4].reshape(NPS * LP, D),
            "sret": X["state_ret"][b], "shg": X["state_hgrn"][b], "cwk": X["cache_win_k"][b], "cwv": X["cache_win_v"][b],
            "cdk": X["cache_diff_k"][b], "cdv": X["cache_diff_v"][b], "cs": X["c"][b], "cctx": X["c_ctx"],
            "norm1_g": X["norm1_g"], "norm2_g": X["norm2_g"], "w_ada": X["w_ada"], "b_ada": X["b_ada"], "w_in": X["w_in"],
            "ret_decay": X["ret_decay"].reshape(2, 8), "ret_norm_g": X["ret_norm_g"], "win_q_norm": X["win_q_norm"],
            "win_k_norm": X["win_k_norm"], "win_sink": X["win_sink"], "diff_q_norm": X["diff_q_norm"],
            "diff_k_norm": X["diff_k_norm"], "diff_lambda": X["diff_lambda"].reshape(2, 128), "diff_norm_g": X["diff_norm_g"],
            "hgrn_lb_logits": X["hgrn_lb_logits"].reshape(2, 512), "hgrn_norm_g": X["hgrn_norm_g"], "w_out": X["w_out"],
            "w_ffn_in": X["w_ffn_in"], "w_ffn_out": X["w_ffn_out"],
        }
        m.update(consts)
        in_maps.append({k: np.ascontiguousarray(v) for k, v in m.items()})
    res = run_bass_kernel_spmd(nc, in_maps, core_ids=list(range(8)))
    R = res.results
    y_p = np.concatenate([R[c]["yp"].reshape(NPS, LP, D) for c in range(8)], axis=0)
    y_s = np.stack([R[c]["ys"] for c in range(4)], axis=0)
    cat = lambda nm: np.concatenate([R[c][nm] for c in range(8)], axis=0)
    outs = (y_p, y_s, cat("o_sret"), cat("o_wk"), cat("o_wv"), cat("o_dk"), cat("o_dv"), cat("o_shg"))
    if debug:
        return outs, R
    return outs
```

```python
import numpy as np
import concourse.bass as bass
import concourse.mybir as mybir
from concourse.ap import AP

F32 = mybir.dt.float32
BF16 = mybir.dt.bfloat16
AF = mybir.ActivationFunctionType
ALU = mybir.AluOpType
AX = mybir.AxisListType


class Buf:
    __slots__ = ("name", "writers", "readers")

    def __init__(self, name):
        self.name = name
        self.writers = {}
        self.readers = {}


class V:
    __slots__ = ("buf", "ap")

    def __init__(self, buf, ap):
        self.buf = buf
        self.ap = ap

    def __getitem__(self, key):
        if isinstance(key, tuple) and any(k is None for k in key):
            ap = self.ap[tuple(k for k in key if k is not None)]
            pos = 0
            for k in key:
                if k is None:
                    ap = ap.unsqueeze(pos)
                    pos += 1
                elif isinstance(k, int):
                    pass
                else:
                    pos += 1
            return V(self.buf, ap)
        return V(self.buf, self.ap[key])

    def re(self, _pat, **kw):
        return V(self.buf, self.ap.rearrange(_pat, **kw))

    def bc(self, shape):
        return V(self.buf, self.ap.broadcast_to(shape))

    def raw(self, offset_elems, dims):
        return V(self.buf, AP(self.ap.tensor, self.ap.offset + offset_elems, dims))

    def bitcast(self, dt):
        return V(self.buf, self.ap.bitcast(dt))

    @property
    def shape(self):
        return self.ap.shape


class Comp:
    def __init__(self, name, sem, unit):
        self.name = name
        self.sem = sem
        self.unit = unit
        self.count = 0


class Issuer:
    def __init__(self, name, eng):
        self.name = name
        self.eng = eng
        self.known = {}
        self.prog = []


class Ctx:
    def __init__(self, nc, es, n_dma_slots=8):
        self.nc = nc
        self.es = es
        self.iss = {}
        self.comp = {}
        for nm, eng in (("pe", nc.tensor), ("act", nc.scalar), ("dve", nc.vector),
                        ("pool", nc.gpsimd), ("sp", nc.sync)):
            self.iss[nm] = Issuer(nm, eng)
        for nm in ("pe", "act", "dve", "pool"):
            sem = es.enter_context(nc.semaphore("s_" + nm))
            self.comp[nm] = Comp(nm, sem, 1)
        self.dma_slots = {}
        self.dma_rr = {}
        for q in ("sp", "pool", "act"):
            sl = []
            for k in range(n_dma_slots):
                sem = es.enter_context(nc.semaphore(f"d_{q}{k}"))
                c = Comp(f"d_{q}{k}", sem, 16)
                self.comp[c.name] = c
                sl.append(c)
            self.dma_slots[q] = sl
            self.dma_rr[q] = 0
        self.n_inst = 0
        self.all_bufs = []
        self.immediate = True
        self.uid = 0
        self.rec = None
        self.t_eng = {}
        self.t_buf = {}

    def sb(self, name, shape, dtype=F32, es=None):
        self.uid += 1
        name = f"{name}_{self.uid}"
        t = (es or self.es).enter_context(self.nc.sbuf_tensor(name, list(shape), dtype))
        b = Buf(name)
        return V(b, t.ap())

    def ps(self, name, shape, dtype=F32):
        t = self.es.enter_context(self.nc.psum_tensor(name, list(shape), dtype))
        b = Buf(name)
        return V(b, t.ap())

    def dram(self, name, shape, dtype=F32, kind="Internal"):
        t = self.nc.dram_tensor(name, list(shape), dtype, kind=kind)
        b = Buf(name)
        return V(b, t.ap())

    def _deps(self, reads, writes):
        deps = {}
        for b in reads:
            for k, c in b.writers.items():
                if deps.get(k, 0) < c:
                    deps[k] = c
        for b in writes:
            for k, c in b.writers.items():
                if deps.get(k, 0) < c:
                    deps[k] = c
            for k, c in b.readers.items():
                if deps.get(k, 0) < c:
                    deps[k] = c
        return deps

    def _emit(self, issuer, comp, fn, reads, writes, extra_deps=None, cost=0.3):
        if self.rec is not None:
            self.rec.append(("op", issuer, comp, fn, reads, writes, cost))
            return
        deps = self._deps(reads, writes)
        if extra_deps:
            for k, c in extra_deps.items():
                if deps.get(k, 0) < c:
                    deps[k] = c
        waits = []
        for k, c in deps.items():
            if issuer.name == "pe" and k == "pe":
                continue
            if issuer.known.get(k, 0) < c:
                issuer.known[k] = c
                cp = self.comp[k]
                waits.append((cp.sem, c * cp.unit))
        comp.count += 1
        cnt = comp.count
        sem, unit = comp.sem, comp.unit

        def thunk(eng, waits=waits, fn=fn, sem=sem, unit=unit):
            for s, v in waits:
                eng.wait_ge(s, v)
            fn(eng).then_inc(sem, unit)

        if self.immediate:
            thunk(issuer.eng)
        else:
            issuer.prog.append(thunk)
        for b in writes:
            b.writers = {comp.name: cnt}
            b.readers = {}
        for b in reads:
            if b in writes:
                continue
            if b.readers.get(comp.name, 0) < cnt:
                b.readers[comp.name] = cnt
        self.n_inst += 1

    def op(self, engname, fn, reads, writes):
        rb = [v.buf for v in reads]
        wb = [v.buf for v in writes]
        n = 1
        try:
            for d in writes[0].ap.shape[1:]:
                n *= int(d)
        except Exception:
            n = 256
        if engname == "pe":
            passes = 4 if (reads and reads[0].ap.dtype == F32) else 1
            cost = max(0.06, n * passes / 2400.0) + 0.03
        elif engname == "act":
            cost = 0.2 + n * 0.0008
        elif engname == "dve":
            cost = 0.1 + n * 0.00115
        else:
            cost = 0.1 + n * 0.0023
        self._emit(self.iss[engname], self.comp[engname], fn, rb, wb, cost=cost)

    def dma(self, q, out, in_, **kw):
        if self.rec is not None:
            self.rec.append(("dma", q, out, in_, kw))
            return
        slots = self.dma_slots[q]
        k = self.dma_rr[q]
        self.dma_rr[q] = (k + 1) % len(slots)
        comp = slots[k]
        extra = {comp.name: comp.count} if comp.count > 0 else None
        o, i = out.ap, in_.ap
        self._emit(self.iss[q], comp, lambda e: e.dma_start(out=o, in_=i, **kw),
                   [in_.buf], [out.buf], extra)

    def mm(self, out, lhsT, rhs, start=True, stop=True, extra_reads=(), **kw):
        o, l, r = out.ap, lhsT.ap, rhs.ap
        reads = [lhsT, rhs] + list(extra_reads)
        self.op("pe", lambda e: e.matmul(o, l, r, start=start, stop=stop, **kw), reads, [out])

    def tr(self, out, in_, ident):
        o, i, d = out.ap, in_.ap, ident.ap
        self.op("pe", lambda e: e.transpose(o, i, d), [in_, ident], [out])

    def act(self, out, in_, func, bias=None, scale=None, accum_out=None, eng="act"):
        o, i = out.ap, in_.ap
        reads = [in_]
        writes = [out]
        kw = {}
        if bias is not None:
            if isinstance(bias, V):
                reads.append(bias)
                kw["bias"] = bias.ap
            else:
                kw["bias"] = bias
        if scale is not None:
            if isinstance(scale, V):
                reads.append(scale)
                kw["scale"] = scale.ap
            else:
                kw["scale"] = scale
        if accum_out is not None:
            writes.append(accum_out)
            kw["accum_out"] = accum_out.ap
        self.op(eng, lambda e: e.activation(o, i, func, **kw), reads, writes)

    def tt(self, eng, out, in0, in1, op, extra_reads=()):
        o, a, b = out.ap, in0.ap, in1.ap
        self.op(eng, lambda e: e.tensor_tensor(o, a, b, op), [in0, in1] + list(extra_reads), [out])

    def ts(self, eng, out, in0, s1, op0, s2=None, op1=None, accum_out=None):
        o, a = out.ap, in0.ap
        reads = [in0]
        writes = [out]
        if isinstance(s1, V):
            reads.append(s1)
            s1 = s1.ap
        if isinstance(s2, V):
            reads.append(s2)
            s2 = s2.ap
        kw = {}
        if op1 is not None:
            kw["op1"] = op1
        if accum_out is not None:
            writes.append(accum_out)
            kw["accum_out"] = accum_out.ap
        self.op(eng, lambda e: e.tensor_scalar(o, a, s1, s2, op0, **kw), reads, writes)

    def stt(self, out, in0, scalar, in1, op0, op1):
        o, a, b = out.ap, in0.ap, in1.ap
        reads = [in0, in1]
        if isinstance(scalar, V):
            reads.append(scalar)
            scalar = scalar.ap
        self.op("dve", lambda e: e.scalar_tensor_tensor(o, a, scalar, b, op0, op1), reads, [out])

    def copy(self, eng, out, in_):
        o, i = out.ap, in_.ap
        if eng == "act":
            self.op(eng, lambda e: e.copy(o, i), [in_], [out])
        else:
            self.op(eng, lambda e: e.tensor_copy(o, i), [in_], [out])

    def memset(self, eng, out, val):
        o = out.ap
        self.op(eng, lambda e: e.memset(o, val), [], [out])

    def reduce(self, eng, out, in_, op, axis=AX.X):
        o, i = out.ap, in_.ap
        self.op(eng, lambda e: e.tensor_reduce(o, i, axis, op), [in_], [out])

    def recip(self, out, in_):
        o, i = out.ap, in_.ap
        self.op("dve", lambda e: e.reciprocal(o, i), [in_], [out])

    def record(self, gen):
        groups = []
        done = False
        while not done:
            self.rec = []
            try:
                next(gen)
            except StopIteration:
                done = True
            if self.rec:
                groups.append(self.rec)
        self.rec = None
        return groups

    def _sim(self, op, commit):
        if op[0] == "dma":
            _, q, out, in_, kw = op
            eng, rb, wb, cost, lat = q, [in_.buf], [out.buf], 0.1, 2.2
        else:
            _, issuer, comp, fn, rb, wb, cost = op
            eng, lat = issuer.name, 0.15
        st = self.t_eng.get(eng, 0.0)
        for b in rb:
            st = max(st, self.t_buf.get(b, 0.0))
        for b in wb:
            st = max(st, self.t_buf.get(b, 0.0))
        if commit:
            self.t_eng[eng] = st + cost
            for b in wb:
                self.t_buf[b] = st + max(cost, 0.0) + lat
        return st

    def schedule(self, chains):
        idx = [0] * len(chains)
        while True:
            best = None
            for i, ch in enumerate(chains):
                if idx[i] >= len(ch):
                    continue
                st = self._sim(ch[idx[i]][0], False)
                if best is None or st < best[0]:
                    best = (st, i)
            if best is None:
                break
            i = best[1]
            grp = chains[i][idx[i]]
            idx[i] += 1
            for op in grp:
                self._sim(op, True)
                if op[0] == "dma":
                    self.dma(op[1], op[2], op[3], **op[4])
                else:
                    self._emit(op[1], op[2], op[3], op[4], op[5], cost=op[6])

    def barrier(self):
        for nm, issuer in self.iss.items():
            waits = []
            for k, cp in self.comp.items():
                if cp.count > 0 and issuer.known.get(k, 0) < cp.count:
                    if nm == "pe" and k == "pe":
                        continue
                    issuer.known[k] = cp.count
                    waits.append((cp.sem, cp.count * cp.unit))

            def thunk(eng, waits=waits):
                for s, v in waits:
                    eng.wait_ge(s, v)
            if self.immediate:
                thunk(issuer.eng)
            else:
                issuer.prog.append(thunk)

    def finish(self):
        sp = self.iss["sp"]
        waits = []
        for k, cp in self.comp.items():
            if cp.count > 0 and sp.known.get(k, 0) < cp.count:
                waits.append((cp.sem, cp.count * cp.unit))

        def fin(eng, waits=waits):
            for s, v in waits:
                eng.wait_ge(s, v)

        if self.immediate:
            fin(sp.eng)
            return
        sp.prog.append(fin)
        nc = self.nc
        with nc.Block() as block:
            @block.sync
            def _(e):
                for t in self.iss["sp"].prog:
                    t(e)

            @block.tensor
            def _(e):
                for t in self.iss["pe"].prog:
                    t(e)

            @block.scalar
            def _(e):
                for t in self.iss["act"].prog:
                    t(e)

            @block.vector
            def _(e):
                for t in self.iss["dve"].prog:
                    t(e)

            @block.gpsimd
            def _(e):
                for t in self.iss["pool"].prog:
                    t(e)
import math
from contextlib import ExitStack
from concourse.bass_utils import run_bass_kernel_spmd

D = 1024; LS = 4096; LP = 256; NPS = 4; NT = 40; NTOK = 5120; DEPTH = 2
INW = 3584; FH = 2816; PAST = 512; EPS = 1e-6
SEQS = [(0, 32, 0, True)] + [(32 + 2 * s, 2, 1, False) for s in range(NPS)]
C_RQ, C_RK, C_RV, C_RG, C_WQ, C_WK, C_WV, C_DQ, C_DK, C_DV, C_HQ, C_HZ, C_HI, C_HG = (
    0, 256, 512, 768, 1024, 1280, 1408, 1536, 1792, 2048, 2304, 2560, 3072, 3328)

IN_SPECS = [
    ("xs", [LS, D]), ("xp", [NPS * LP, D]), ("sret", [2, 2, 4, 64, 64]), ("shg", [2, 2, 4, 64, 64]),
    ("cwk", [2, 2, 512, 64]), ("cwv", [2, 2, 512, 64]), ("cdk", [2, 4, 2, 512, 32]), ("cdv", [2, 4, 512, 64]),
    ("cs", [D]), ("cctx", [D]), ("norm1_g", [2, D]), ("norm2_g", [2, D]), ("w_ada", [2, D, 6 * D]),
    ("b_ada", [2, 6 * D]), ("w_in", [2, D, INW]), ("ret_decay", [2, 8]), ("ret_norm_g", [2, 256]),
    ("win_q_norm", [2, 64]), ("win_k_norm", [2, 64]), ("win_sink", [2, 4]), ("diff_q_norm", [2, 32]),
    ("diff_k_norm", [2, 32]), ("diff_lambda", [2, 128]), ("diff_norm_g", [2, 256]),
    ("hgrn_lb_logits", [2, 512]), ("hgrn_norm_g", [2, 256]), ("w_out", [2, D, D]),
    ("w_ffn_in", [2, D, 2 * FH]), ("w_ffn_out", [2, FH, D]),
    ("ident", [128, 128]), ("rope64", [LS, 128]), ("rope32", [LS, 64]), ("tmask", [128, 3, 128]),
    ("chunkind", [128, 4]), ("wmask", [128, 4, 512]),
]
OUT_SPECS = [
    ("yp", [NPS * LP, D]), ("ys", [LS, D]), ("o_sret", [NPS, 2, 2, 4, 64, 64]), ("o_wk", [NPS, 2, 2, LP, 64]),
    ("o_wv", [NPS, 2, 2, LP, 64]), ("o_dk", [NPS, 2, 4, 2, LP, 32]), ("o_dv", [NPS, 2, 4, LP, 64]),
    ("o_shg", [NPS, 2, 2, 4, 64, 64]),
]


class _Stop(Exception):
    pass


def build(debug=False, nlayers=DEPTH, stage=99):
    nc = bass.Bass("TRN2", target_bir_lowering=False)
    I = {}
    for nm, shp in IN_SPECS:
        I[nm] = V(Buf(nm), nc.dram_tensor(nm, shp, F32, kind="ExternalInput").ap())
    O = {}
    for nm, shp in OUT_SPECS:
        O[nm] = V(Buf(nm), nc.dram_tensor(nm, shp, F32, kind="ExternalOutput").ap())
    es = ExitStack()
    if True:
        cx = Ctx(nc, es, n_dma_slots=16)
        dk = "ExternalOutput" if debug else "Internal"
        X1 = cx.dram("X1", [NTOK, D], F32, kind=dk)
        X2 = cx.dram("X2", [NTOK, D], F32, kind=dk)
        OCAT = cx.dram("OCAT", [NTOK, D], BF16, kind=dk)
        QKT = {m: cx.dram(f"QKT{m}", [NT, 128, 8, 128], BF16) for m in (0, 1)}
        UU = {m: cx.dram(f"UU{m}", [NT, 128, 1024], BF16) for m in (0, 1)}
        VV = {m: cx.dram(f"VV{m}", [NT, 128, 256], BF16) for m in (0, 1)}
        GG = {m: cx.dram(f"GG{m}", [NT, 128, 256], BF16) for m in (0, 1)}
        DECD = cx.dram("DECD", [NT, 128, 16], F32)
        QTB = cx.dram("QTB", [128, NTOK // 256, 2, 256], BF16)
        KTB = cx.dram("KTB", [128, NTOK], BF16)
        VB = cx.dram("VB", [NTOK, 128], BF16)
        QTC = cx.dram("QTC", [128, 2, NTOK], BF16)
        KTC = cx.dram("KTC", [128, 2, NTOK], BF16)
        VC = cx.dram("VC", [NTOK, 256], BF16)

        ps = [cx.ps(f"ps{i}", [128, 512]) for i in range(8)]

        def psb(i):
            return V(ps[i].buf, ps[i].ap.tensor.bitcast(BF16).ap())

        ident = cx.sb("ident", [128, 128]); identb = cx.sb("identb", [128, 128], BF16)
        tmask = cx.sb("tmask", [128, 3, 128]); tmaskb = cx.sb("tmaskb", [128, 2, 128], BF16)
        chunkind = cx.sb("chunkind", [128, 4]); wmaskb = cx.sb("wmaskb", [128, 4, 512], BF16)
        cx.dma("sp", ident, I["ident"]); cx.dma("sp", tmask, I["tmask"]); cx.dma("sp", chunkind, I["chunkind"])
        cx.dma("pool", wmaskb, I["wmask"])
        cx.copy("dve", identb, ident)
        cx.copy("dve", tmaskb, tmask[:, 0:2, :])
        cT = cx.sb("cT", [128, 8, 2]); sil = cx.sb("sil", [128, 8, 2], BF16)
        cx.dma("sp", cT[:, :, 0], I["cs"].re("(kc p) -> p kc", p=128), allow_slow_non_contiguous=True)
        cx.dma("sp", cT[:, :, 1], I["cctx"].re("(kc p) -> p kc", p=128), allow_slow_non_contiguous=True)
        cx.act(sil, cT, AF.Silu)

        def bcast_load(dst, src_ap_v):
            cx.dma("sp", dst, V(src_ap_v.buf, src_ap_v.ap.partition_broadcast(128)))

        def rstd_from_ss(dst, ss, n, scope):
            cx.act(dst, ss, AF.Ln, scale=1.0 / n, bias=EPS)
            cx.act(dst, dst, AF.Exp, scale=-0.5)

        def chk(sidx):
            if stage == sidx:
                raise _Stop()

        for l in range(nlayers):
          try:
            lam_init = 0.8 - 0.6 * math.exp(-0.3 * l)
            last = (l == DEPTH - 1)
            with ExitStack() as LSC:
                A1 = cx.sb("A1", [128, 8, 2], es=LSC); SH1 = cx.sb("SH1", [128, 8, 2], es=LSC)
                A2 = cx.sb("A2", [128, 8, 2], es=LSC); SH2 = cx.sb("SH2", [128, 8, 2], es=LSC)
                G1bc = cx.sb("G1bc", [128, 2, D], es=LSC); G2bc = cx.sb("G2bc", [128, 2, D], es=LSC)
                with ExitStack() as S:
                    wbuf = [cx.sb("wada", [128, 8, 512], BF16, es=S) for _ in range(2)]
                    badaT = cx.sb("badaT", [128, 48], es=S)
                    bbc = cx.sb("bbc", [128, 2, D], es=S)
                    n1T = cx.sb("n1T", [128, 8], es=S); n2T = cx.sb("n2T", [128, 8], es=S)
                    modT = cx.sb("modT", [128, 4, 8, 2], es=S)
                    silrep = cx.sb("silrep", [128, 8, 2, 128], BF16, es=S)
                    cx.copy("dve", silrep, sil[:, :, :, None].bc([128, 8, 2, 128]))
                    cx.dma("sp", badaT, I["b_ada"][l].re("(c p) -> p c", p=128), allow_slow_non_contiguous=True)
                    cx.dma("sp", n1T, I["norm1_g"][l].re("(c p) -> p c", p=128), allow_slow_non_contiguous=True)
                    cx.dma("sp", n2T, I["norm2_g"][l].re("(c p) -> p c", p=128), allow_slow_non_contiguous=True)
                    bcast_load(bbc[:, 0, :], I["b_ada"][l, 2 * D:3 * D])
                    bcast_load(bbc[:, 1, :], I["b_ada"][l, 5 * D:6 * D])
                    for j in range(12):
                        wb = wbuf[j % 2]
                        cx.dma("pool", wb, I["w_ada"][l, :, j * 512:(j + 1) * 512].re("(kc p) n -> p kc n", p=128))
                        v, half = j // 2, j % 2
                        if v in (2, 5):
                            gi = 0 if v == 2 else 1
                            for g in range(2):
                                pt = ps[(j + g) % 2]
                                for kc in range(8):
                                    cx.mm(pt, silrep[:, kc, g, :], wb[:, kc, :], start=(kc == 0), stop=(kc == 7))
                                dst = (G1bc if gi == 0 else G2bc)[:, g, half * 512:(half + 1) * 512]
                                cx.tt("dve", dst, pt, bbc[:, gi, half * 512:(half + 1) * 512], ALU.add)
                        else:
                            vi = {0: 0, 1: 1, 3: 2, 4: 3}[v]
                            pt = ps[2 + (j % 2)]
                            for oc in range(4):
                                for kc in range(8):
                                    cx.mm(pt[:, oc * 2:(oc + 1) * 2], wb[:, kc, oc * 128:(oc + 1) * 128], sil[:, kc, :],
                                          start=(kc == 0), stop=(kc == 7))
                            c0 = v * 8 + half * 4
                            cx.tt("dve", modT[:, vi, half * 4:(half + 1) * 4, :], pt[:, 0:8].re("p (c g) -> p c g", g=2),
                                  badaT[:, c0:c0 + 4, None].bc([128, 4, 2]), ALU.add)
                    for (Adst, SHdst, nT, ish, isc) in ((A1, SH1, n1T, 0, 1), (A2, SH2, n2T, 2, 3)):
                        cx.ts("dve", Adst, modT[:, isc], 1.0, ALU.add)
                        cx.tt("dve", Adst, Adst, nT[:, :, None].bc([128, 8, 2]), ALU.mult)
                        cx.copy("dve", SHdst, modT[:, ish])
                cx.barrier()
                chk(0)

                with ExitStack() as S:
                    win = cx.sb("win", [128, 8, INW], BF16, es=S)
                    for blk in range(7):
                        cx.dma("pool", win[:, :, blk * 512:(blk + 1) * 512],
                               I["w_in"][l, :, blk * 512:(blk + 1) * 512].re("(kc p) n -> p kc n", p=128))
                    gw = cx.sb("gw", [128, 6, 64], es=S); gd = cx.sb("gd", [128, 16, 32], es=S)
                    t64 = cx.sb("t64", [128, 2, 64], es=S); t32 = cx.sb("t32", [128, 2, 32], es=S)
                    bcast_load(t64[:, 0, :], I["win_q_norm"][l]); bcast_load(t64[:, 1, :], I["win_k_norm"][l])
                    bcast_load(t32[:, 0, :], I["diff_q_norm"][l]); bcast_load(t32[:, 1, :], I["diff_k_norm"][l])
                    cx.copy("dve", gw[:, 0:4, :], t64[:, 0:1, :].bc([128, 4, 64]))
                    cx.copy("dve", gw[:, 4:6, :], t64[:, 1:2, :].bc([128, 2, 64]))
                    cx.copy("dve", gd[:, 0:8, :], t32[:, 0:1, :].bc([128, 8, 32]))
                    cx.copy("dve", gd[:, 8:16, :], t32[:, 1:2, :].bc([128, 8, 32]))
                    gret = cx.sb("gret", [128, 256], es=S); ghg = cx.sb("ghg", [128, 256], es=S)
                    bcast_load(gret, I["ret_norm_g"][l]); bcast_load(ghg, I["hgrn_norm_g"][l])
                    LB = cx.sb("LB", [128, 512], es=S); OMLB = cx.sb("OMLB", [128, 512], es=S)
                    if l == 0:
                        cx.memset("pool", LB, 0.0); cx.memset("pool", OMLB, 1.0)
                    else:
                      with ExitStack() as S3:
                        lg = cx.sb("lblog", [128, 2, 512], es=S3); mx = cx.sb("lbmx", [128, 512], es=S3)
                        for ll in range(2):
                            bcast_load(lg[:, ll, :], I["hgrn_lb_logits"][ll])
                        cx.tt("dve", mx, lg[:, 0, :], lg[:, 1, :], ALU.max)
                        cx.tt("dve", lg, lg, mx[:, None, :].bc([128, 2, 512]), ALU.subtract)
                        cx.act(lg, lg, AF.Exp)
                        cx.tt("dve", mx, lg[:, 0, :], lg[:, 1, :], ALU.add)
                        cx.recip(mx, mx)
                        cx.tt("dve", LB.re("p (h r d) -> p r h d", h=4, r=2), lg[:, 0, :].re("p (r h d) -> p r h d", r=2, h=4),
                              mx.re("p (r h d) -> p r h d", r=2, h=4), ALU.mult)
                        cx.ts("dve", OMLB, LB, -1.0, ALU.mult, 1.0, ALU.add)
                        cx.barrier()

                    cumT = cx.sb("cumT", [128, 512], es=S)

                    def tables(lf, E, Ei, W, DEC, scope):
                        cum = cumT
                        cx.mm(ps[4], tmask[:, 0, :], lf)
                        cx.mm(ps[5], tmask[:, 1, :], lf)
                        cx.mm(ps[6], tmask[:, 2, :], lf)
                        c4 = cum.re("p (h r d) -> p h r d", h=4, r=2)
                        cx.copy("act", c4[:, :, 0, :], ps[4].re("p (h r d) -> p h r d", h=4, r=2)[:, :, 0, :])
                        cx.copy("act", c4[:, :, 1, :], ps[5].re("p (h r d) -> p h r d", h=4, r=2)[:, :, 1, :])
                        cx.tt("dve", W, ps[6], cum, ALU.subtract)
                        if DEC is not None:
                            for h in range(4):
                                cx.mm(ps[7][:, h * 4:(h + 1) * 4], lf[:, h * 128:(h + 1) * 128], chunkind)
                            cx.act(DEC, ps[7][:, 0:16], AF.Exp)
                        yield
                        cx.act(E, cum, AF.Exp)
                        yield
                        cx.act(Ei, cum, AF.Exp, scale=-1.0)
                        yield
                        cx.act(W, W, AF.Exp)
                        yield

                    Er = cx.sb("Er", [128, 512], es=S); Eir = cx.sb("Eir", [128, 512], es=S)
                    Wr = cx.sb("Wr", [128, 512], es=S); DECr = cx.sb("DECr", [128, 8], es=S)
                    with ExitStack() as S2:
                        lfr = cx.sb("lfr", [128, 8, 64], es=S2); rd8 = cx.sb("rd8", [128, 8], es=S2)
                        bcast_load(rd8, I["ret_decay"][l])
                        cx.act(rd8, rd8, AF.Exp)
                        cx.ts("dve", rd8, rd8, -1.0, ALU.mult)
                        cx.copy("dve", lfr.re("p (h r) d -> p r h d", r=2), rd8.re("p (r h) -> p r h", r=2)[:, :, :, None].bc([128, 2, 4, 64]))
                        for _ in tables(lfr.re("p a d -> p (a d)"), Er, Eir, Wr, None, S2):
                            pass
                        cx.barrier()
                    cx.ts("dve", Eir, Eir, 0.125, ALU.mult)
                    cx.ts("dve", Wr, Wr, 0.125, ALU.mult)
                    xt = [cx.sb("xt", [128, D], es=S) for _ in range(2)]
                    rp64 = [cx.sb("rp64", [128, 128], es=S) for _ in range(2)]
                    rp32 = [cx.sb("rp32", [128, 64], es=S) for _ in range(2)]
                    junk = cx.sb("junk", [128, D], BF16, es=S); xn = cx.sb("xn", [128, D], es=S)
                    ss1 = cx.sb("ss1", [128, 1], es=S); rs1 = cx.sb("rs1", [128, 1], es=S)
                    hT = cx.sb("hT", [128, 8, 128], BF16, es=S)
                    proj = cx.sb("proj", [128, INW], es=S)
                    ssw = cx.sb("ssw", [128, 6], es=S); ssd = cx.sb("ssd", [128, 16], es=S)
                    rsw = cx.sb("rsw", [128, 6], es=S); rsd = cx.sb("rsd", [128, 16], es=S)
                    nw = cx.sb("nw", [128, 6, 64], es=S); nd = cx.sb("nd", [128, 16, 32], es=S)
                    nw2 = cx.sb("nw2", [128, 6, 64], es=S); nd2 = cx.sb("nd2", [128, 16, 32], es=S)
                    rqk = cx.sb("rqk", [128, 8, 64], es=S)
                    r1 = cx.sb("r1", [128, 512], es=S); r2 = cx.sb("r2", [128, 512], es=S)
                    nwb = cx.sb("nwb", [128, 384], BF16, es=S); ndb = cx.sb("ndb", [128, 512], BF16, es=S)
                    tst = cx.sb("tst", [128, 7, 128], BF16, es=S)
                    vbb = cx.sb("vbb", [128, 128], BF16, es=S); vcb = cx.sb("vcb", [128, 256], BF16, es=S)
                    sig = cx.sb("sig", [128, 512], es=S); ff = sig
                    lfh = cx.sb("lfh", [128, 512], es=S); kh = cx.sb("kh", [128, 512], es=S)
                    Eh = cx.sb("Eh", [128, 512], es=S); Eih = cx.sb("Eih", [128, 512], es=S)
                    Wh = cx.sb("Wh", [128, 512], es=S); DECh = cx.sb("DECh", [128, 16], es=S)
                    Qp = cx.sb("Qp", [128, 512], BF16, es=S); Kp = cx.sb("Kp", [128, 512], BF16, es=S)
                    Kpp = cx.sb("Kpp", [128, 512], BF16, es=S)
                    vb16 = cx.sb("vb16", [128, 256], BF16, es=S); gsil = cx.sb("gsil", [128, 256], es=S)
                    gb16 = cx.sb("gb16", [128, 256], BF16, es=S)
                    qkst = cx.sb("qkst", [128, 8, 128], BF16, es=S); ust = cx.sb("ust", [128, 4, 256], BF16, es=S)
                    cumS = ExitStack(); S.enter_context(cumS)
                    _s1 = (cx.sb("Qp1", [128, 512], BF16, es=S), cx.sb("Kp1", [128, 512], BF16, es=S),
                           cx.sb("Kpp1", [128, 512], BF16, es=S), cx.sb("vb161", [128, 256], BF16, es=S),
                           cx.sb("gsil1", [128, 256], es=S), cx.sb("gb161", [128, 256], BF16, es=S),
                           cx.sb("qkst1", [128, 8, 128], BF16, es=S), cx.sb("ust1", [128, 4, 256], BF16, es=S))
                    _s2 = (cx.sb("Qp2", [128, 512], BF16, es=S), cx.sb("Kp2", [128, 512], BF16, es=S),
                           cx.sb("Kpp2", [128, 512], BF16, es=S), cx.sb("vb162", [128, 256], BF16, es=S),
                           _s1[4], cx.sb("gb162", [128, 256], BF16, es=S), _s1[6], _s1[7])
                    _s0 = (Qp, Kp, Kpp, vb16, gsil, gb16, qkst, ust)
                    _s0b = (cx.sb("Qp0b", [128, 512], BF16, es=S), cx.sb("Kp0b", [128, 512], BF16, es=S),
                            cx.sb("Kpp0b", [128, 512], BF16, es=S), cx.sb("vb160b", [128, 256], BF16, es=S),
                            _s0[4], cx.sb("gb160b", [128, 256], BF16, es=S), _s0[6], _s0[7])
                    rp_tmp = {(0, 0): _s0, (0, 1): _s0b, (1, 0): _s1, (1, 1): _s2}
                    DEChs = [DECh, cx.sb("DECh2", [128, 16], es=S)]

                    def xsrc(t):
                        if l == 0:
                            return I["xs"][t * 128:(t + 1) * 128, :] if t < 32 else I["xp"][(t - 32) * 128:(t - 31) * 128, :]
                        return X2[t * 128:(t + 1) * 128, :]

                    r1b = cx.sb("r1b", [128, 512], es=S); r2b = cx.sb("r2b", [128, 512], es=S)

                    def rope(dst, src, tab, H, Dh, r1=r1, r2=r2):
                        hd = Dh // 4
                        Cb = tab[:, None, 0:Dh].bc([128, H, Dh])
                        cx.tt("dve", r1[:, 0:H * Dh].re("p (h d) -> p h d", h=H), src, Cb, ALU.mult)
                        yield
                        s5 = src.re("p h (a s d) -> p h a s d", a=2, s=2)
                        S5 = tab[:, Dh:2 * Dh].re("p (a s d) -> p a s d", a=2, s=2)
                        o5 = r2[:, 0:H * Dh].re("p (h a s d) -> p h a s d", h=H, a=2, s=2)
                        for sidx in range(2):
                            cx.tt("pool", o5[:, :, :, sidx, :], s5[:, :, :, 1 - sidx, :],
                                  S5[:, None, :, sidx, :].bc([128, H, 2, hd]), ALU.mult)
                            yield
                        cx.tt("dve", dst, r1[:, 0:H * Dh].re("p (h d) -> p h d", h=H),
                              r2[:, 0:H * Dh].re("p (h d) -> p h d", h=H), ALU.add)
                        yield

                    def rec_postA(m, t, q, kfb, v, gatesrc, gtab, E, Ei, W):
                        v4 = lambda a: a.re("p (h r d) -> p h r d", h=4, r=2)
                        Qp, Kp, Kpp, vb16, gsil, gb16, qkst, ust = rp_tmp[(m, t % 2)]
                        qb = q.re("p (h d) -> p h d", h=4)[:, :, None, :].bc([128, 4, 2, 64])
                        cx.tt("dve", v4(Qp), v4(E), qb, ALU.mult)
                        yield
                        if kfb.shape[1] == 256:
                            kb = kfb.re("p (h d) -> p h d", h=4)[:, :, None, :].bc([128, 4, 2, 64])
                        else:
                            kb = v4(kfb)
                        cx.tt("pool", v4(Kp), v4(Ei), kb, ALU.mult)
                        yield
                        cx.tt("pool", v4(Kpp), v4(W), kb, ALU.mult)
                        yield
                        cx.copy("act", vb16, v)
                        yield
                        cx.act(gsil, gatesrc, AF.Exp, scale=-1.0)
                        yield
                        cx.act(gsil, gsil, AF.Ln, bias=1.0)
                        yield
                        cx.act(gsil, gsil, AF.Exp, scale=-1.0)
                        yield
                        cx.tt("dve", gsil, gsil, gatesrc, ALU.mult)
                        yield
                        cx.tt("pool", gb16, gsil, gtab, ALU.mult)
                        yield

                    def rec_postB(m, t, DEC):
                        Qp, Kp, Kpp, vb16, gsil, gb16, qkst, ust = rp_tmp[(m, t % 2)]
                        pb = psb(7)
                        for h in range(4):
                            cx.tr(pb[:, h * 128:(h + 1) * 128], Qp[:, h * 128:(h + 1) * 128], identb)
                            cx.tr(pb[:, (4 + h) * 128:(5 + h) * 128], Kp[:, h * 128:(h + 1) * 128], identb)
                        cx.copy("act", qkst.re("p a b -> p (a b)"), pb)
                        yield
                        cx.dma("sp", QKT[m][t], qkst)
                        yield
                        for c in range(4):
                            kwm = {"tile_position": (96, 0)} if c == 3 else {}
                            for h in range(4):
                                cx.mm(ps[4 + c][:, h * 64:(h + 1) * 64], Kpp[c * 32:(c + 1) * 32, h * 128:(h + 1) * 128],
                                      vb16[c * 32:(c + 1) * 32, h * 64:(h + 1) * 64], **kwm)
                        for c in range(4):
                            cx.copy("dve" if c % 2 == 0 else "act", ust[:, c, :], ps[4 + c][:, 0:256])
                        yield
                        cx.dma("sp", UU[m][t], ust.re("p c x -> p (c x)"))
                        yield
                        cx.dma("sp", VV[m][t], vb16)
                        yield
                        cx.dma("sp", GG[m][t], gb16)
                        yield
                        if DEC is not None:
                            cx.dma("sp", DECD[t], DEC)
                            yield

                    def p1_load(t):
                        b = t % 2
                        cx.dma("sp", xt[b], xsrc(t))
                        if t < 32:
                            cx.dma("sp", rp64[b], I["rope64"][t * 128:(t + 1) * 128, :])
                            cx.dma("sp", rp32[b], I["rope32"][t * 128:(t + 1) * 128, :])

                    projs = [proj, cx.sb("projB", [128, INW], es=S)]
                    xns = [xn, xn]
                    hTs = [hT, cx.sb("hTB", [128, 8, 128], BF16, es=S)]
                    ss1s = [ss1, cx.sb("ss1B", [128, 1], es=S)]; rs1s = [rs1, cx.sb("rs1B", [128, 1], es=S)]

                    def front(t):
                        g = 0 if t < 32 else 1
                        x = xt[t % 2]; xn_ = xns[t % 2]; hT_ = hTs[t % 2]; pj = projs[t % 2]
                        cx.act(junk, x, AF.Square, accum_out=ss1s[t % 2])
                        rstd_from_ss(rs1s[t % 2], ss1s[t % 2], D, cumS)
                        cx.ts("dve", xn_, x, rs1s[t % 2], ALU.mult)
                        yield
                        for kc in range(8):
                            cx.tr(ps[kc // 4][:, (kc % 4) * 128:(kc % 4 + 1) * 128], xn_[:, kc * 128:(kc + 1) * 128], ident)
                        for kc in range(8):
                            cx.act(hT_[:, kc, :], ps[kc // 4][:, (kc % 4) * 128:(kc % 4 + 1) * 128], AF.Identity,
                                   scale=A1[:, kc, g:g + 1], bias=SH1[:, kc, g:g + 1])
                        for blk in range(7):
                            pt = ps[2 + blk % 2]
                            for kc in range(8):
                                cx.mm(pt, hT_[:, kc, :], win[:, kc, blk * 512:(blk + 1) * 512], start=(kc == 0), stop=(kc == 7))
                            cx.copy("act" if blk % 2 == 0 else "dve", pj[:, blk * 512:(blk + 1) * 512], pt)
                            yield

                    p1_load(0)
                    for _ in front(0):
                        pass
                    fgen = [None]

                    def adv(n=1):
                        for _ in range(n):
                            if fgen[0] is not None:
                                next(fgen[0], None)

                    nwbs = [nwb, cx.sb("nwb2", [128, 384], BF16, es=S)]; ndbs = [ndb, cx.sb("ndb2", [128, 512], BF16, es=S)]

                    def chain_bc(t):
                        g = 0 if t < 32 else 1
                        latent = t < 32
                        proj = projs[t % 2]
                        pw = proj[:, C_WQ:C_WQ + 384].re("p (h d) -> p h d", h=6)
                        pd = proj[:, C_DQ:C_DQ + 512].re("p (h d) -> p h d", h=16)
                        cx.tt("pool", nw2, pw, pw, ALU.mult)
                        yield
                        cx.tt("pool", nd2, pd, pd, ALU.mult)
                        yield
                        cx.reduce("dve", ssw, nw2, ALU.add)
                        yield
                        cx.reduce("dve", ssd, nd2, ALU.add)
                        yield
                        rstd_from_ss(rsw, ssw, 64, cumS)
                        rstd_from_ss(rsd, ssd, 32, cumS)
                        cx.tt("dve", nw, pw, rsw[:, :, None].bc([128, 6, 64]), ALU.mult)
                        yield
                        cx.tt("pool", nw, nw, gw, ALU.mult)
                        yield
                        cx.tt("dve", nd, pd, rsd[:, :, None].bc([128, 16, 32]), ALU.mult)
                        yield
                        cx.tt("pool", nd, nd, gd, ALU.mult)
                        yield
                        if not latent:
                            sq = 1 + (t - 32) // 2; bl = sq - 1; lt = (t - 32) % 2
                            tsl = slice(lt * 128, (lt + 1) * 128)
                            cx.dma("sp", O["o_wk"][bl, l, :, tsl, :].re("k t d -> t k d"), nw[:, 4:6, :])
                            yield
                            cx.dma("sp", O["o_wv"][bl, l, :, tsl, :].re("k t d -> t k d"),
                                   proj[:, C_WV:C_WV + 128].re("p (k d) -> p k d", k=2))
                            yield
                            cx.dma("sp", O["o_dk"][bl, l, :, :, tsl, :].re("h c t d -> t (h c) d"), nd[:, 8:16, :])
                            yield
                            cx.dma("sp", O["o_dv"][bl, l, :, tsl, :].re("h t d -> t h d"),
                                   proj[:, C_DV:C_DV + 256].re("p (h d) -> p h d", h=4))
                            yield
                            nwr, ndr = nw, nd
                        else:
                            yield from rope(nw2, nw, rp64[t % 2], 6, 64)
                            yield from rope(nd2, nd, rp32[t % 2], 16, 32)
                            nwr, ndr = nw2, nd2
                        nwb_ = nwbs[t % 2]; ndb_ = ndbs[t % 2]
                        tk = slice(t * 128, (t + 1) * 128)
                        cx.copy("act", vbb, proj[:, C_WV:C_WV + 128])
                        yield
                        cx.copy("act", vcb, proj[:, C_DV:C_DV + 256])
                        yield
                        cx.dma("sp", VB[tk, :], vbb)
                        yield
                        cx.dma("sp", VC[tk, :], vcb)
                        yield
                        cx.copy("pool", nwb_[:, 0:256].re("p (g k d) -> p k g d", g=2, k=2), nwr[:, 0:4, :].re("p (k g) d -> p k g d", k=2))
                        yield
                        cx.copy("act", nwb_[:, 256:384], nwr[:, 4:6, :].re("p h d -> p (h d)"))
                        yield
                        cx.copy("act", ndb_, ndr.re("p h d -> p (h d)"))
                        yield

                    def chain_bcB(t):
                        nwb_ = nwbs[t % 2]; ndb_ = ndbs[t % 2]
                        tk = slice(t * 128, (t + 1) * 128)
                        pb = psb(4)
                        for gq in range(2):
                            cx.tr(pb[:, gq * 128:(gq + 1) * 128], nwb_[:, gq * 128:(gq + 1) * 128], identb)
                        cx.tr(pb[:, 256:384], nwb_[:, 256:384], identb)
                        for j in range(4):
                            cx.tr(pb[:, 384 + j * 128:512 + j * 128], ndb_[:, j * 128:(j + 1) * 128], identb)
                        cx.copy("act", tst.re("p a b -> p (a b)"), pb[:, 0:896])
                        yield
                        cx.dma("sp", QTB[:, t // 2, :, (t % 2) * 128:(t % 2 + 1) * 128], tst[:, 0:2, :])
                        yield
                        cx.dma("sp", KTB[:, tk], tst[:, 2, :])
                        yield
                        cx.dma("sp", QTC[:, :, tk], tst[:, 3:5, :])
                        yield
                        cx.dma("sp", KTC[:, :, tk], tst[:, 5:7, :])
                        yield

                    def chain_ret(t):
                        g = 0 if t < 32 else 1
                        latent = t < 32
                        proj = projs[t % 2]
                        if latent:
                            yield from rope(rqk, proj[:, 0:512].re("p (h d) -> p h d", h=8), rp64[t % 2], 8, 64, r1b, r2b)
                            rqkr = rqk
                        else:
                            rqkr = proj[:, 0:512].re("p (h d) -> p h d", h=8)
                        yield from rec_postA(0, t, rqkr[:, 0:4, :].re("p h d -> p (h d)"), rqkr[:, 4:8, :].re("p h d -> p (h d)"), proj[:, C_RV:C_RV + 256],
                                 proj[:, C_RG:C_RG + 256], gret, Er, Eir, Wr)
                        yield

                    def chain_hg(t):
                        g = 0 if t < 32 else 1
                        latent = t < 32
                        proj = projs[t % 2]
                        cx.act(sig.re("p (h r d) -> p r h d", h=4, r=2), proj[:, C_HZ:C_HZ + 512].re("p (r h d) -> p r h d", r=2, h=4), AF.Exp, scale=-1.0)
                        yield
                        cx.act(sig, sig, AF.Ln, bias=1.0)
                        yield
                        cx.act(sig, sig, AF.Exp, scale=-1.0)
                        yield
                        cx.tt("dve", ff, sig, OMLB, ALU.mult)
                        yield
                        cx.tt("dve", ff, ff, LB, ALU.add)
                        yield
                        cx.ts("dve", ff, ff, 1e-30, ALU.max)
                        yield
                        cx.act(lfh, ff, AF.Ln)
                        yield
                        cx.ts("pool", kh, ff, -1.0, ALU.mult, 1.0, ALU.add)
                        yield
                        yield from tables(lfh, Eh, Eih, Wh, DEChs[t % 2], cumS)
                        yield from rec_postA(1, t, proj[:, C_HQ:C_HQ + 256], kh, proj[:, C_HI:C_HI + 256],
                                 proj[:, C_HG:C_HG + 256], ghg, Eh, Eih, Wh)
                        yield

                    for t in range(NT):
                        chains = []
                        if t + 1 < NT:
                            p1_load(t + 1)
                            chains.append(cx.record(front(t + 1)))
                        chains += [cx.record(chain_hg(t)), cx.record(chain_bc(t)), cx.record(chain_ret(t))]
                        if t >= 1:
                            chains.append(cx.record(rec_postB(1, t - 1, DEChs[(t - 1) % 2])))
                            chains.append(cx.record(chain_bcB(t - 1)))
                            chains.append(cx.record(rec_postB(0, t - 1, None)))
                        cx.schedule(chains)
                    for _ in rec_postB(1, NT - 1, DEChs[(NT - 1) % 2]):
                        pass
                    for _ in chain_bcB(NT - 1):
                        pass
                    for _ in rec_postB(0, NT - 1, None):
                        pass
                cx.barrier()
                chk(1)

                for m in (0, 1):
                    with ExitStack() as S:
                        Sbf = cx.sb("Sbf", [128, 4 * NT, 256], BF16, es=S)
                        SbfF = V(Buf("SbfF"), Sbf.ap); SbfB = V(Buf("SbfB"), Sbf.ap)
                        decr = None
                        if m == 0:
                            decr = cx.sb("decr", [128, 8], es=S)
                            bcast_dummy = None
                            rdT = cx.sb("rdT", [128, 4], es=S)
                            cx.dma("sp", rdT[0:64, :], V(I["ret_decay"].buf, I["ret_decay"].ap[l, 0:4].partition_broadcast(64)))
                            cx.dma("sp", rdT[64:128, :], V(I["ret_decay"].buf, I["ret_decay"].ap[l, 4:8].partition_broadcast(64)))
                            cx.act(rdT, rdT, AF.Exp)
                            cx.act(decr[:, 0:4], rdT, AF.Exp, scale=-32.0)
                        for (t0, ntl, g, latent) in SEQS:
                            nch = 4 * ntl
                            with ExitStack() as S2:
                                Ua = cx.sb("Ua", [128, ntl, 1024], BF16, es=S2)
                                cx.dma("sp", Ua, UU[m][t0:t0 + ntl].re("t p x -> p t x"))
                                Ua4 = Ua.re("p t (c x) -> p (t c) x", c=4)
                                Da = None
                                if m == 1:
                                    Da = cx.sb("Da", [128, ntl, 16], es=S2)
                                    cx.dma("sp", Da, DECD[t0:t0 + ntl].re("t p x -> p t x"))
                                St = cx.sb("St", [128, 256], es=S2); Tm = cx.sb("Tm", [128, 256], es=S2)
                                StF = V(Buf("StF"), St.ap[0:64]); StB = V(Buf("StB"), St.ap[64:128])
                                TmF = V(Buf("TmF"), Tm.ap[0:64]); TmB = V(Buf("TmB"), Tm.ap[64:128])
                                src_state = I["sret"] if m == 0 else I["shg"]
                                if latent:
                                    for r in range(2):
                                        cx.dma("sp", (StF if r == 0 else StB).re("p (h v) -> p h v", h=4),
                                               src_state[l, r].re("h d v -> d h v"))
                                else:
                                    cx.memset("dve", StF, 0.0); cx.memset("pool", StB, 0.0)
                                for s in range(nch):
                                    for r, (eng, Sx, Tx, Sb) in enumerate((("dve", StF, TmF, SbfF), ("pool", StB, TmB, SbfB))):
                                        n = s if r == 0 else nch - 1 - s
                                        prt = slice(r * 64, (r + 1) * 64)
                                        gch = 4 * t0 + n
                                        cx.copy("act", Sb[prt, gch, :], Sx)
                                        if m == 0:
                                            dv_ = decr[prt, 0:4, None].bc([64, 4, 64])
                                        else:
                                            tt_, cc_ = n // 4, n % 4
                                            dv_ = Da[prt, tt_, :].re("p (h c) -> p h c", c=4)[:, :, cc_:cc_ + 1].bc([64, 4, 64])
                                        cx.tt(eng, Tx.re("p (h v) -> p h v", h=4), Sx.re("p (h v) -> p h v", h=4), dv_, ALU.mult)
                                        cx.tt(eng, Sx, Tx, Ua4[prt, n, :], ALU.add)
                                if not latent:
                                    bl = (t0 - 32) // 2
                                    dst = O["o_sret"] if m == 0 else O["o_shg"]
                                    for r in range(2):
                                        cx.dma("sp", dst[bl, l, r].re("h d v -> d h v"),
                                               (StF if r == 0 else StB).re("p (h v) -> p h v", h=4))
                            cx.barrier()
                        qk = [cx.sb("qk", [128, 8, 128], BF16, es=S) for _ in range(4)]
                        vv = [cx.sb("vv", [128, 256], BF16, es=S) for _ in range(4)]
                        gg = [cx.sb("gg", [128, 256], BF16, es=S) for _ in range(4)]
                        Afs = [cx.sb("Af", [128, 512], BF16, es=S) for _ in range(2)]; Ab = cx.sb("Ab", [128, 512], BF16, es=S)
                        sq_ = cx.sb("sq", [128, 256], es=S); ss4 = cx.sb("ss4", [128, 4], es=S); rs4 = cx.sb("rs4", [128, 4], es=S)
                        on_ = cx.sb("on", [128, 256], es=S); of_ = [cx.sb("of", [128, 256], BF16, es=S) for _ in range(2)]
                        cumS = ExitStack(); S.enter_context(cumS)

                        def p2_load(t):
                            b = t % 4
                            cx.dma("sp", qk[b], QKT[m][t]); cx.dma("sp", vv[b], VV[m][t]); cx.dma("sp", gg[b], GG[m][t])
                        def p2_A(t):
                            b = t % 2
                            Q = qk[t % 4][:, 0:4, :]; K = qk[t % 4][:, 4:8, :]
                            Af = Afs[b]
                            pf = ps[0 + 2 * b]; pbk = ps[1 + 2 * b]
                            for h in range(4):
                                cx.mm(pf[:, h * 128:(h + 1) * 128], K[0:64, h, :], Q[0:64, h, :])
                                cx.mm(pbk[:, h * 128:(h + 1) * 128], K[64:128, h, :], Q[64:128, h, :])
                            cx.tt("dve", Af.re("p (h i) -> p h i", h=4), pf.re("p (h i) -> p h i", h=4),
                                  tmaskb[:, 0:1, :].bc([128, 4, 128]), ALU.mult)
                            cx.tt("dve", Ab.re("p (h i) -> p h i", h=4), pbk.re("p (h i) -> p h i", h=4),
                                  tmaskb[:, 1:2, :].bc([128, 4, 128]), ALU.mult)
                            cx.tt("pool", Af, Af, Ab, ALU.add)

                        def p2_B2(t):
                            b = t % 2
                            po = ps[4 + t % 2]
                            cx.act(sq_, po[:, 0:256], AF.Square)
                            cx.reduce("dve", ss4, sq_.re("p (h d) -> p h d", h=4), ALU.add)
                            rstd_from_ss(rs4, ss4, 64, cumS)
                            cx.tt("dve", on_.re("p (h d) -> p h d", h=4), po[:, 0:256].re("p (h d) -> p h d", h=4),
                                  rs4[:, :, None].bc([128, 4, 64]), ALU.mult)
                            cx.tt("pool", of_[b], on_, gg[t % 4], ALU.mult)
                            co = 0 if m == 0 else 768
                            cx.dma("sp", OCAT[t * 128:(t + 1) * 128, co:co + 256], of_[b])

                        p2_load(0)
                        if NT > 1:
                            p2_load(1)
                        p2_A(0)
                        for t in range(NT):
                            b = t % 2
                            if t + 2 < NT:
                                p2_load(t + 2)
                            if t + 1 < NT:
                                p2_A(t + 1)
                            Q = qk[t % 4][:, 0:4, :]; K = qk[t % 4][:, 4:8, :]
                            Af = Afs[b]
                            po = ps[4 + t % 2]
                            for h in range(4):
                                cx.mm(po[:, h * 64:(h + 1) * 64], Af[:, h * 128:(h + 1) * 128], vv[t % 4][:, h * 64:(h + 1) * 64],
                                      start=True, stop=False, skip_group_check=True)
                                for c in range(4):
                                    kwm = {"tile_position": (0, 96)} if c == 3 else {}
                                    cx.mm(po[c * 32:(c + 1) * 32, h * 64:(h + 1) * 64], Q[:, h, c * 32:(c + 1) * 32],
                                          Sbf[:, 4 * t + c, h * 64:(h + 1) * 64], start=False, stop=(c == 3),
                                          extra_reads=(SbfF, SbfB), skip_group_check=True, **kwm)
                            if t >= 1:
                                p2_B2(t - 1)
                        p2_B2(NT - 1)
                    cx.barrier()

                chk(2)
                with ExitStack() as SWA:
                    wfi_a = cx.sb("wfi_a", [128, 8, 3072], BF16, es=SWA)
                    for (t0, ntl, g, latent) in [(0, 32, 0, True), (32, 2 * NPS, 1, False)]:
                        L = ntl * 128
                        nkt = ntl + (4 if latent else 0)
                        tok0 = t0 * 128
                        with ExitStack() as S:
                            qtb = cx.sb("qtb", [128, 2, L // 256, 512], BF16, es=S); ktb = cx.sb("ktb", [128, nkt * 128], BF16, es=S)
                            vbt = cx.sb("vbt", [128, nkt, 2, 65], BF16, es=S)
                            cx.memset("pool", qtb, 0.0)
                            for kvq in range(2):
                                cx.dma("sp", qtb[kvq * 64:(kvq + 1) * 64, kvq], QTB[kvq * 64:(kvq + 1) * 64, tok0 // 256:(tok0 + L) // 256].re("p b g q -> p b (g q)"))
                            cx.dma("sp", ktb[:, 0:L], KTB[:, tok0:tok0 + L])
                            cx.memset("pool", vbt[:, :, :, 64:65], 1.0)
                            for kt in range(ntl):
                                cx.dma("sp", vbt[:, kt, :, 0:64], VB[tok0 + kt * 128:tok0 + (kt + 1) * 128, :].re("p (k d) -> p k d", k=2))
                            if latent:
                                ck = cx.sb("ck", [128, 4, 2, 64], es=S)
                                for kt in range(4):
                                    cx.dma("sp", ck[:, kt], I["cwk"][l, :, kt * 128:(kt + 1) * 128, :].re("k p d -> p k d"))
                                    cx.dma("pool", vbt[:, ntl + kt, :, 0:64], I["cwv"][l, :, kt * 128:(kt + 1) * 128, :].re("k p d -> p k d"))
                                for kt in range(4):
                                    cx.tr(ps[0][:, kt * 128:(kt + 1) * 128], ck[:, kt].re("p k d -> p (k d)"), ident)
                                cx.copy("act", ktb[:, L:L + 512], ps[0])
                            esk = cx.sb("esk", [128, 4], es=S); eskv = cx.sb("eskv", [128, 2, 2, 2], es=S)
                            bcast_load(esk, I["win_sink"][l])
                            cx.act(esk, esk, AF.Exp)
                            cx.copy("dve", eskv, esk.re("p (k g) -> p k g", k=2)[:, :, :, None].bc([128, 2, 2, 2]))
                            Pt = [cx.sb("Pt", [128, 512], BF16, es=S) for _ in range(4)]
                            den = cx.sb("den", [128, 4], es=S); ob = [cx.sb("ob", [128, 2, 256], BF16, es=S) for _ in range(2)]
                            pi = 0
                            for qb in range(L // 256):
                                a = 2 * qb
                                if latent:
                                    keys = [(a + o, o + 1) for o in (-1, 0, 1, 2) if 0 <= a + o < ntl] + [(ntl + k, None) for k in range(4)]
                                else:
                                    keys = [(2 * qb, None), (2 * qb + 1, None)]
                                obq = ob[qb % 2]
                                its = [(kv, ki, kt, mo) for kv in range(2) for ki, (kt, mo) in enumerate(keys)]

                                def b_score(i):
                                    kv, ki, kt, mo = its[i]
                                    cx.mm(ps[i % 3], ktb[:, kt * 128:(kt + 1) * 128], qtb[:, kv, qb, :])

                                b_score(0)
                                if len(its) > 1:
                                    b_score(1)
                                for i, (kv, ki, kt, mo) in enumerate(its):
                                    po = ps[4 + kv + 2 * (qb % 2)]
                                    if i + 2 < len(its):
                                        b_score(i + 2)
                                    P = Pt[pi % 4]; pi += 1
                                    cx.act(P, ps[i % 3], AF.Exp, scale=0.125)
                                    if mo is not None:
                                        cx.tt("dve", P, P, wmaskb[:, mo, :], ALU.mult)
                                    for gq in range(2):
                                        for u in range(2):
                                            j = gq * 2 + u
                                            cx.mm(po[:, j * 65:(j + 1) * 65], P[:, gq * 256 + u * 128:gq * 256 + (u + 1) * 128],
                                                  vbt[:, kt, kv, :], start=(ki == 0 and j == 0), stop=(ki == len(keys) - 1),
                                                  skip_group_check=True)
                                    if ki == len(keys) - 1:
                                        cx.tt("dve", den, po[:, 64:260:65], eskv[:, kv].re("p g u -> p (g u)"), ALU.add)
                                        cx.recip(den, den)
                                        o4 = obq.re("p u (k g d) -> p k g u d", k=2, g=2)[:, kv]
                                        cx.tt("dve", o4, po[:, 0:260].re("p (g u e) -> p g u e", g=2, u=2)[:, :, :, 0:64],
                                              den.re("p (g u) -> p g u", g=2)[:, :, :, None].bc([128, 2, 2, 64]), ALU.mult)
                                cx.dma("sp", OCAT[tok0 + qb * 256:tok0 + (qb + 1) * 256, 256:512].re("(u p) c -> p u c", p=128), obq)
                        cx.barrier()
                        with ExitStack() as S:
                            qtc = cx.sb("qtc", [128, 4, 2, L], BF16, es=S); ktc = cx.sb("ktc", [128, 2, nkt * 128], BF16, es=S)
                            vct = cx.sb("vct", [128, nkt, 4, 65], BF16, es=S)
                            cx.memset("pool", qtc, 0.0)
                            for vq in range(4):
                                cx.dma("sp", qtc[vq * 32:(vq + 1) * 32, vq, :, :], QTC[vq * 32:(vq + 1) * 32, :, tok0:tok0 + L])
                            cx.dma("sp", ktc[:, :, 0:L], KTC[:, :, tok0:tok0 + L])
                            cx.memset("pool", vct[:, :, :, 64:65], 1.0)
                            for kt in range(ntl):
                                cx.dma("sp", vct[:, kt, :, 0:64], VC[tok0 + kt * 128:tok0 + (kt + 1) * 128, :].re("p (h d) -> p h d", h=4))
                            if latent:
                                ck = cx.sb("ckc", [128, 4, 8, 32], es=S)
                                for kt in range(4):
                                    cx.dma("sp", ck[:, kt], I["cdk"][l, :, :, kt * 128:(kt + 1) * 128, :].re("h c p d -> p (h c) d"))
                                    cx.dma("pool", vct[:, ntl + kt, :, 0:64], I["cdv"][l, :, kt * 128:(kt + 1) * 128, :].re("h p d -> p h d"))
                                for hp in range(2):
                                    for kt in range(4):
                                        cx.tr(ps[hp][:, kt * 128:(kt + 1) * 128], ck[:, kt, hp * 4:(hp + 1) * 4, :].re("p a d -> p (a d)"), ident)
                                    cx.copy("act", ktc[:, hp, L:L + 512], ps[hp])
                            dl = cx.sb("dl", [128, 128], es=S); dp = cx.sb("dp", [128, 2, 32], es=S); d2 = cx.sb("d2", [128, 2], es=S)
                            lam = cx.sb("lam", [128, 1], es=S)
                            bcast_load(dl, I["diff_lambda"][l])
                            dl4 = dl.re("p (a b d) -> p a b d", a=2, b=2)
                            cx.tt("dve", dp, dl4[:, :, 0, :], dl4[:, :, 1, :], ALU.mult)
                            cx.reduce("dve", d2, dp, ALU.add)
                            cx.act(d2, d2, AF.Exp)
                            cx.tt("dve", lam, d2[:, 0:1], d2[:, 1:2], ALU.subtract)
                            cx.ts("dve", lam, lam, float(lam_init), ALU.add)
                            gdn = cx.sb("gdn", [128, 256], es=S)
                            bcast_load(gdn, I["diff_norm_g"][l])
                            cx.ts("dve", gdn, gdn, float(1.0 - lam_init), ALU.mult)
                            if latent:
                                for blk in range(6):
                                    cx.dma("pool", wfi_a[:, :, blk * 512:(blk + 1) * 512],
                                           I["w_ffn_in"][l, :, blk * 512:(blk + 1) * 512].re("(kc p) n -> p kc n", p=128))
                            QB = 512 if latent else 256; nsub = QB // 128
                            Pt = [cx.sb("Ptc", [128, 512], BF16, es=S) for _ in range(4)]
                            dn1 = cx.sb("dn1", [128, 4], es=S); dn2 = cx.sb("dn2", [128, 4], es=S)
                            t1_ = cx.sb("t1c", [128, 4, 64], es=S); t2_ = cx.sb("t2c", [128, 4, 64], es=S)
                            od = cx.sb("od", [128, 4, 4, 64], es=S)
                            sqd = cx.sb("sqd", [128, 4, 4, 64], es=S); ssq = cx.sb("ssq", [128, 16], es=S); rsq = cx.sb("rsq", [128, 16], es=S)
                            odb = [cx.sb("odb", [128, 4, 256], BF16, es=S) for _ in range(2)]
                            cumS = ExitStack(); S.enter_context(cumS)
                            pi = 0
                            for qb in range(L // QB):
                                kts = list(range(nkt)) if latent else [2 * qb, 2 * qb + 1]
                                its = [(h, c, kt) for h in range(4) for c in range(2) for kt in kts]

                                def c_score(i):
                                    h, c, kt = its[i]
                                    hp, hh = h // 2, h % 2
                                    cx.mm(ps[i % 3][:, 0:QB], ktc[:, hp, kt * 128:(kt + 1) * 128], qtc[:, hh * 2 + c, hp, qb * QB:(qb + 1) * QB])

                                c_score(0)
                                c_score(1)
                                for i, (h, c, kt) in enumerate(its):
                                    pA = ps[4 + 2 * (h % 2)]; pB = ps[5 + 2 * (h % 2)]
                                    po = pA if c == 0 else pB
                                    if i + 2 < len(its):
                                        c_score(i + 2)
                                    P = Pt[pi % 4]; pi += 1
                                    cx.act(P[:, 0:QB], ps[i % 3][:, 0:QB], AF.Exp, scale=32 ** -0.5)
                                    for u in range(nsub):
                                        cx.mm(po[:, u * 65:(u + 1) * 65], P[:, u * 128:(u + 1) * 128], vct[:, kt, h, :],
                                              start=(kt == kts[0] and u == 0), stop=(kt == kts[-1]), skip_group_check=True)
                                    if c == 1 and kt == kts[-1]:
                                        e1 = nsub * 65
                                        cx.recip(dn1[:, 0:nsub], pA[:, 64:e1:65])
                                        cx.recip(dn2[:, 0:nsub], pB[:, 64:e1:65])
                                        cx.ts("dve", dn2[:, 0:nsub], dn2[:, 0:nsub], lam[:, 0:1], ALU.mult)
                                        cx.tt("dve", t1_[:, 0:nsub, :], pA[:, 0:e1].re("p (u e) -> p u e", u=nsub)[:, :, 0:64],
                                              dn1[:, 0:nsub, None].bc([128, nsub, 64]), ALU.mult)
                                        cx.tt("dve", t2_[:, 0:nsub, :], pB[:, 0:e1].re("p (u e) -> p u e", u=nsub)[:, :, 0:64],
                                              dn2[:, 0:nsub, None].bc([128, nsub, 64]), ALU.mult)
                                        cx.tt("pool", od[:, 0:nsub, h, :], t1_[:, 0:nsub, :], t2_[:, 0:nsub, :], ALU.subtract)
                                nn = nsub * 4
                                odv = od[:, 0:nsub].re("p u h d -> p (u h) d")
                                sqv = sqd[:, 0:nsub].re("p u h d -> p (u h) d")
                                cx.tt("pool", sqv, odv, odv, ALU.mult)
                                cx.reduce("dve", ssq[:, 0:nn], sqv, ALU.add)
                                rstd_from_ss(rsq[:, 0:nn], ssq[:, 0:nn], 64, cumS)
                                cx.tt("dve", sqv, odv, rsq[:, 0:nn, None].bc([128, nn, 64]), ALU.mult)
                                ob_ = odb[qb % 2]
                                cx.tt("pool", ob_[:, 0:nsub, :], sqd[:, 0:nsub].re("p u h d -> p u (h d)"),
                                      gdn[:, None, :].bc([128, nsub, 256]), ALU.mult)
                                cx.dma("sp", OCAT[tok0 + qb * QB:tok0 + (qb + 1) * QB, 512:768].re("(u p) c -> p u c", p=128), ob_[:, 0:nsub, :])
                        cx.barrier()

                    chk(3)
                    with ExitStack() as SF:
                        wfi_b = cx.sb("wfi_b", [128, 8, 2 * FH - 3072], BF16, es=SF)
                        with ExitStack() as S:
                            wout = cx.sb("wout", [128, 8, D], BF16, es=S)
                            for blk in range(2):
                                cx.dma("pool", wout[:, :, blk * 512:(blk + 1) * 512],
                                       I["w_out"][l, :, blk * 512:(blk + 1) * 512].re("(kc p) n -> p kc n", p=128))
                            for blk in range(5):
                                cx.dma("pool", wfi_b[:, :, blk * 512:(blk + 1) * 512],
                                       I["w_ffn_in"][l, :, 3072 + blk * 512:3072 + (blk + 1) * 512].re("(kc p) n -> p kc n", p=128))
                            oc = [cx.sb("oc", [128, D], BF16, es=S) for _ in range(3)]
                            xt = [cx.sb("xt3", [128, D], es=S) for _ in range(3)]
                            ocTs = [cx.sb("ocT", [128, 8, 128], BF16, es=S) for _ in range(2)]
                            tmps = [cx.sb("tmp3", [128, D], es=S) for _ in range(2)]; xo = [cx.sb("xo3", [128, D], es=S) for _ in range(2)]

                            def p3_load(t):
                                cx.dma("sp", oc[t % 3], OCAT[t * 128:(t + 1) * 128, :])
                                cx.dma("sp", xt[t % 3], xsrc(t))

                            def p3_A(t):
                                pb = psb(t % 2)
                                for kc in range(8):
                                    cx.tr(pb[:, kc * 128:(kc + 1) * 128], oc[t % 3][:, kc * 128:(kc + 1) * 128], identb)
                                cx.copy("act", ocTs[t % 2].re("p a b -> p (a b)"), pb)

                            p3_load(0)
                            if NT > 1:
                                p3_load(1)
                            p3_A(0)
                            for t in range(NT):
                                g = 0 if t < 32 else 1
                                if t + 2 < NT:
                                    p3_load(t + 2)
                                if t + 1 < NT:
                                    p3_A(t + 1)
                                ocT = ocTs[t % 2]; tmp = tmps[t % 2]
                                for nb in range(2):
                                    pt = ps[2 + nb + 2 * (t % 2)]
                                    for kc in range(8):
                                        cx.mm(pt, ocT[:, kc, :], wout[:, kc, nb * 512:(nb + 1) * 512], start=(kc == 0), stop=(kc == 7))
                                    cx.tt("dve", tmp[:, nb * 512:(nb + 1) * 512], pt, G1bc[:, g, nb * 512:(nb + 1) * 512], ALU.mult)
                                cx.tt("pool", xo[t % 2], tmp, xt[t % 3], ALU.add)
                                cx.dma("sp", X1[t * 128:(t + 1) * 128, :], xo[t % 2])
                        cx.barrier()

                        chk(4)
                        with ExitStack() as S:
                            wfo = cx.sb("wfo", [128, 22, D], BF16, es=S)
                            for blk in range(2):
                                cx.dma("pool", wfo[:, :, blk * 512:(blk + 1) * 512],
                                       I["w_ffn_out"][l, :, blk * 512:(blk + 1) * 512].re("(j p) n -> p j n", p=128))
                            TB = 2
                            xb = [cx.sb("xb", [128, TB, D], es=S) for _ in range(2)]
                            junk = cx.sb("junk4", [128, D], es=S); xn = cx.sb("xn4", [128, D], es=S)
                            ss1 = cx.sb("ss14", [128, 1], es=S); rs1 = cx.sb("rs14", [128, 1], es=S)
                            hT2 = cx.sb("hT2", [128, 8, TB * 128], BF16, es=S)
                            hidt = cx.sb("hid", [128, 22, TB * 128], BF16, es=S)
                            hid = [V(Buf(f"hid{j}"), hidt.ap[:, j, :]) for j in range(22)]
                            sg = [cx.sb("sg", [128, TB * 128], BF16, es=S) for _ in range(2)]
                            tmp = cx.sb("tmp4", [128, D], es=S)
                            cumS = ExitStack(); S.enter_context(cumS)
                            NB = NT // TB
                            ydst = None

                            def p4_load(bi):
                                cx.dma("sp", xb[bi % 2], X1[bi * TB * 128:(bi + 1) * TB * 128, :].re("(u p) c -> p u c", p=128))
                            p4_load(0)
                            for bi in range(NB):
                                t0 = bi * TB
                                g = 0 if t0 < 32 else 1
                                if bi + 1 < NB:
                                    p4_load(bi + 1)
                                xbb = xb[bi % 2]
                                for u in range(TB):
                                    cx.act(junk, xbb[:, u, :], AF.Square, accum_out=ss1)
                                    rstd_from_ss(rs1, ss1, D, cumS)
                                    cx.ts("dve", xn, xbb[:, u, :], rs1, ALU.mult)
                                    for kc in range(8):
                                        cx.tr(ps[kc // 4][:, (kc % 4) * 128:(kc % 4 + 1) * 128], xn[:, kc * 128:(kc + 1) * 128], ident)
                                    for kc in range(8):
                                        cx.act(hT2[:, kc, u * 128:(u + 1) * 128], ps[kc // 4][:, (kc % 4) * 128:(kc % 4 + 1) * 128], AF.Identity,
                                               scale=A2[:, kc, g:g + 1], bias=SH2[:, kc, g:g + 1])
                                for j in range(22):
                                    pg = ps[2 + (j % 2)]; pu = ps[4 + (j % 2)]
                                    for kc in range(8):
                                        cx.mm(pg[:, 0:TB * 128], wfi_a[:, kc, j * 128:(j + 1) * 128], hT2[:, kc, :], start=(kc == 0), stop=(kc == 7))
                                    for kc in range(8):
                                        cx.mm(pu[:, 0:TB * 128], (wfi_a[:, kc, FH + j * 128:FH + (j + 1) * 128] if j < 2 else
                                                                  wfi_b[:, kc, FH + j * 128 - 3072:FH + (j + 1) * 128 - 3072]), hT2[:, kc, :], start=(kc == 0), stop=(kc == 7))
                                    cx.act(sg[j % 2], pg[:, 0:TB * 128], AF.Silu)
                                    cx.tt("dve", hid[j], sg[j % 2], pu[:, 0:TB * 128], ALU.mult)
                                for u in range(TB):
                                    t = t0 + u
                                    for nb in range(2):
                                        pt = ps[6 + nb]
                                        for j in range(22):
                                            cx.mm(pt, hid[j][:, u * 128:(u + 1) * 128], wfo[:, j, nb * 512:(nb + 1) * 512], start=(j == 0), stop=(j == 21))
                                        cx.tt("dve", tmp[:, nb * 512:(nb + 1) * 512], pt, G2bc[:, g, nb * 512:(nb + 1) * 512], ALU.mult)
                                    cx.tt("pool", xbb[:, u, :], tmp, xbb[:, u, :], ALU.add)
                                    if last:
                                        dst = O["ys"][t * 128:(t + 1) * 128, :] if t < 32 else O["yp"][(t - 32) * 128:(t - 31) * 128, :]
                                    else:
                                        dst = X2[t * 128:(t + 1) * 128, :]
                                    cx.dma("sp", dst, xbb[:, u, :])
                        cx.barrier()
          except _Stop:
            cx.barrier()
            break
        cx.finish()
    try:
        es.close()
    except Exception as e:
        if stage == 99:
            raise
    print("n_inst", cx.n_inst)
    return nc


def rope_tables():
    def tab(Dh):
        dax = Dh // 2; half = dax // 2
        inv = 10000.0 ** (-np.arange(half, dtype=np.float32) / half)
        t = np.arange(LS)
        row = (t // 64).astype(np.float32); col = (t % 64).astype(np.float32)
        out = np.zeros((LS, 2 * Dh), np.float32)
        for ai, pos in enumerate((row, col)):
            ang = (pos[:, None] * inv[None, :]).astype(np.float32)
            c, s = np.cos(ang), np.sin(ang)
            b = ai * dax
            out[:, b:b + half] = c; out[:, b + half:b + dax] = c
            out[:, Dh + b:Dh + b + half] = -s; out[:, Dh + b + half:Dh + b + dax] = s
        return out
    return tab(64), tab(32)


def const_inputs():
    r64, r32 = rope_tables()
    j = np.arange(128)[:, None]; i = np.arange(128)[None, :]
    same = (j // 32) == (i // 32)
    tm = np.stack([(j <= i) & same, (j >= i) & same, same], axis=1).astype(np.float32)
    ci = np.zeros((128, 4), np.float32)
    for c in range(4):
        ci[c * 32:(c + 1) * 32, c] = 1
    MU = (j >= i).astype(np.float32); ML = (j <= i).astype(np.float32); ON = np.ones((128, 128), np.float32); Z = np.zeros((128, 128), np.float32)
    wm = np.zeros((128, 4, 2, 2, 128), np.float32)
    for mo in range(4):
        o = mo - 1
        for u in range(2):
            rel = o - u
            blk = {-2: Z, -1: MU, 0: ON, 1: ML, 2: Z}[rel]
            wm[:, mo, :, u, :] = blk[:, None, :]
    return {"ident": np.eye(128, dtype=np.float32), "rope64": r64, "rope32": r32, "tmask": tm, "chunkind": ci,
            "wmask": wm.reshape(128, 4, 512)}


_NC_CACHE = {}


def kernel(**inputs):
    debug = bool(inputs.pop("_debug", False))
    nlayers = int(inputs.pop("_nlayers", DEPTH))
    stage = int(inputs.pop("_stage", 99))
    f = lambda a: np.ascontiguousarray(np.asarray(a, dtype=np.float32))
    X = {k: f(v) for k, v in inputs.items()}
    consts = const_inputs()
    key = (debug, nlayers, stage)
    if key not in _NC_CACHE:
        _NC_CACHE[key] = build(debug, nlayers, stage)
    nc = _NC_CACHE[key]
    in_maps = []
    for core in range(8):
        b = core % 4
        m = {
            "xs": X["x_sample"][b], "xp": X["x_prompt"][core * 4:(core + 1) * 4].reshape(NPS * LP, D),
            "sret": X["state_ret"][b], "shg": X["state_hgrn"][b], "cwk": X["cache_win_k"][b], "cwv": X["cache_win_v"][b],
            "cdk": X["cache_diff_k"][b], "cdv": X["cache_diff_v"][b], "cs": X["c"][b], "cctx": X["c_ctx"],
            "norm1_g": X["norm1_g"], "norm2_g": X["norm2_g"], "w_ada": X["w_ada"], "b_ada": X["b_ada"], "w_in": X["w_in"],
            "ret_decay": X["ret_decay"].reshape(2, 8), "ret_norm_g": X["ret_norm_g"], "win_q_norm": X["win_q_norm"],
            "win_k_norm": X["win_k_norm"], "win_sink": X["win_sink"], "diff_q_norm": X["diff_q_norm"],
            "diff_k_norm": X["diff_k_norm"], "diff_lambda": X["diff_lambda"].reshape(2, 128), "diff_norm_g": X["diff_norm_g"],
            "hgrn_lb_logits": X["hgrn_lb_logits"].reshape(2, 512), "hgrn_norm_g": X["hgrn_norm_g"], "w_out": X["w_out"],
            "w_ffn_in": X["w_ffn_in"], "w_ffn_out": X["w_ffn_out"],
        }
        m.update(consts)
        in_maps.append({k: np.ascontiguousarray(v) for k, v in m.items()})
    res = run_bass_kernel_spmd(nc, in_maps, core_ids=list(range(8)))
    R = res.results
    y_p = np.concatenate([R[c]["yp"].reshape(NPS, LP, D) for c in range(8)], axis=0)
    y_s = np.stack([R[c]["ys"] for c in range(4)], axis=0)
    cat = lambda nm: np.concatenate([R[c][nm] for c in range(8)], axis=0)
    outs = (y_p, y_s, cat("o_sret"), cat("o_wk"), cat("o_wv"), cat("o_dk"), cat("o_dv"), cat("o_shg"))
    if debug:
        return outs, R
    return outs
```
